# Optimizing a Trainium2 kernel written in Bass

```python
import jax, jax.numpy as jnp
from jax import lax
import numpy as np

D_MODEL = 1024
BATCH = 16
SEQ = 4096
DEPTH = 1
DEC_BATCH = 16
DEC_SEQ = 16
PAST_LEN = 1024

CHUNK = 64
N_HEADS = 16
N_KV_HEADS = 4
HEAD_DIM = 64
Q_GROUP = N_HEADS // N_KV_HEADS
WINDOW = 128
WINDOW_CHUNKS = WINDOW // CHUNK
ATTN_W = N_HEADS * HEAD_DIM
KV_W = N_KV_HEADS * HEAD_DIM
GMLP_CHUNK = 128
GMLP_GROUPS = 4
GMLP_W = D_MODEL
GMLP_GW = GMLP_W // GMLP_GROUPS
D_FF = ((8 * D_MODEL // 3 + 255) // 256) * 256
IN_W = ATTN_W + 2 * KV_W + 2 * GMLP_W + 2 * D_MODEL
EPS = 1e-6
NEG_INF = -1e30

kernel_name = "chunk_causal_swa_sink_gmlp_hybrid_step"


def rms_norm(x, g):
    xf = x.astype(jnp.float32)
    y = xf * lax.rsqrt(jnp.mean(xf * xf, axis=-1, keepdims=True) + EPS)
    return (y * g.astype(jnp.float32)).astype(x.dtype)


def layer_norm(x, g, b):
    xf = x.astype(jnp.float32)
    mu = jnp.mean(xf, axis=-1, keepdims=True)
    xc = xf - mu
    y = xc * lax.rsqrt(jnp.mean(xc * xc, axis=-1, keepdims=True) + EPS)
    return (y * g.astype(jnp.float32) + b.astype(jnp.float32)).astype(x.dtype)


def sink_attention(q, k, v, sinks, mask):
    s = jnp.einsum('...qhgd,...khd->...hgqk', q, k).astype(jnp.float32) * (HEAD_DIM ** -0.5)
    if mask is not None:
        s = jnp.where(mask, s, NEG_INF)
    sk = sinks.astype(jnp.float32)[:, :, None, None]
    m = jnp.maximum(jnp.max(s, axis=-1, keepdims=True), sk)
    p = jnp.exp(s - m)
    denom = jnp.sum(p, axis=-1, keepdims=True) + jnp.exp(sk - m)
    p = (p / denom).astype(v.dtype)
    return jnp.einsum('...hgqk,...khd->...qhgd', p, v)


def attn_prompt(q, k, v, sinks):
    B, S = q.shape[:2]
    nc = S // CHUNK
    pad = WINDOW_CHUNKS * CHUNK
    qc = q.reshape(B, nc, CHUNK, N_KV_HEADS, Q_GROUP, HEAD_DIM)
    kp = jnp.pad(k, ((0, 0), (pad, 0), (0, 0), (0, 0))).reshape(B, nc + WINDOW_CHUNKS, CHUNK, N_KV_HEADS, HEAD_DIM)
    vp = jnp.pad(v, ((0, 0), (pad, 0), (0, 0), (0, 0))).reshape(B, nc + WINDOW_CHUNKS, CHUNK, N_KV_HEADS, HEAD_DIM)
    kb = jnp.concatenate([kp[:, i:i + nc] for i in range(WINDOW_CHUNKS + 1)], axis=2)
    vb = jnp.concatenate([vp[:, i:i + nc] for i in range(WINDOW_CHUNKS + 1)], axis=2)
    kpos = jnp.arange(nc)[:, None] * CHUNK - pad + jnp.arange((WINDOW_CHUNKS + 1) * CHUNK)[None, :]
    mask = (kpos >= 0)[:, None, None, None, :]
    o = sink_attention(qc, kb, vb, sinks, mask)
    return o.reshape(B, S, ATTN_W)


def attn_sample(q, k, v, cache_k, cache_v, sinks):
    B, T = q.shape[:2]
    kc = jnp.concatenate([cache_k, k], axis=1)
    vc = jnp.concatenate([cache_v, v], axis=1)
    o = sink_attention(q.reshape(B, T, N_KV_HEADS, Q_GROUP, HEAD_DIM), kc, vc, sinks, None)
    w = cache_k.shape[1]
    return o.reshape(B, T, ATTN_W), kc[:, -w:], vc[:, -w:]


def spatial_gate(u, v, w_s, b_s):
    L = v.shape[-3]
    wm = jnp.tril(w_s[:, :L, :L])
    mix = jnp.einsum('gts,...sgc->...tgc', wm, v) + b_s[:, :L].T[:, :, None]
    return u * mix


def trunk_layer(x, n_pre_mix, n_post_mix, n_pre_ffn, n_post_ffn, w_in, sinks, ln_g, ln_b, w_s, b_s,
                w_attn_proj, w_gmlp_proj, w_out, w_ffn_in, w_ffn_out, cache_k, cache_v):
    B, T, _ = x.shape
    h = rms_norm(x, n_pre_mix)
    z = h @ w_in
    offs = np.cumsum([ATTN_W, KV_W, KV_W, 2 * GMLP_W, D_MODEL])
    q, k, v, zg, ga, gb = jnp.split(z, offs, axis=-1)
    k = k.reshape(B, T, N_KV_HEADS, HEAD_DIM)
    v = v.reshape(B, T, N_KV_HEADS, HEAD_DIM)
    if cache_k is None:
        a = attn_prompt(q, k, v, sinks)
        new_k, new_v = k[:, -WINDOW:], v[:, -WINDOW:]
    else:
        a, new_k, new_v = attn_sample(q, k, v, cache_k, cache_v, sinks)
    u, gv = jnp.split(jax.nn.gelu(zg), 2, axis=-1)
    gv = layer_norm(gv, ln_g, ln_b)
    if cache_k is None:
        shp = (B, T // GMLP_CHUNK, GMLP_CHUNK, GMLP_GROUPS, GMLP_GW)
    else:
        shp = (B, T, GMLP_GROUPS, GMLP_GW)
    bm = spatial_gate(u.reshape(shp), gv.reshape(shp), w_s, b_s).reshape(B, T, GMLP_W)
    merged = jax.nn.sigmoid(ga) * (a @ w_attn_proj) + jax.nn.sigmoid(gb) * (bm @ w_gmlp_proj)
    x = x + rms_norm(merged @ w_out, n_post_mix)
    h2 = rms_norm(x, n_pre_ffn)
    g, up = jnp.split(h2 @ w_ffn_in, 2, axis=-1)
    x = x + rms_norm((jax.nn.silu(g) * up) @ w_ffn_out, n_post_ffn)
    return x, new_k, new_v, gv


def setup_inputs(seed: int = 0) -> dict:
    key = jax.random.key(seed)
    ks = jax.random.split(key, 20)
    cache_w = min(WINDOW, PAST_LEN)
    nrm = lambda k, shp, s: jax.random.normal(k, shp, jnp.float32) * s
    return {
        "x_prompt": nrm(ks[0], (BATCH, SEQ, D_MODEL), 1.0),
        "x_sample": nrm(ks[1], (DEC_BATCH, DEC_SEQ, D_MODEL), 1.0),
        "cache_attn_k": nrm(ks[2], (DEPTH, DEC_BATCH, cache_w, N_KV_HEADS, HEAD_DIM), 1.0),
        "cache_attn_v": nrm(ks[3], (DEPTH, DEC_BATCH, cache_w, N_KV_HEADS, HEAD_DIM), 1.0),
        "norm_pre_mix": 1.0 + nrm(ks[4], (DEPTH, D_MODEL), 0.05),
        "norm_post_mix": 1.0 + nrm(ks[5], (DEPTH, D_MODEL), 0.05),
        "norm_pre_ffn": 1.0 + nrm(ks[6], (DEPTH, D_MODEL), 0.05),
        "norm_post_ffn": 1.0 + nrm(ks[7], (DEPTH, D_MODEL), 0.05),
        "w_in": nrm(ks[8], (DEPTH, D_MODEL, IN_W), D_MODEL ** -0.5),
        "attn_sinks": nrm(ks[9], (DEPTH, N_KV_HEADS, Q_GROUP), 1.0),
        "gmlp_ln_g": 1.0 + nrm(ks[10], (DEPTH, GMLP_W), 0.05),
        "gmlp_ln_b": nrm(ks[11], (DEPTH, GMLP_W), 0.02),
        "gmlp_w_s": nrm(ks[12], (DEPTH, GMLP_GROUPS, GMLP_CHUNK, GMLP_CHUNK), GMLP_CHUNK ** -0.5),
        "gmlp_b_s": 1.0 + nrm(ks[13], (DEPTH, GMLP_GROUPS, GMLP_CHUNK), 0.1),
        "w_attn_proj": nrm(ks[14], (DEPTH, ATTN_W, D_MODEL), ATTN_W ** -0.5),
        "w_gmlp_proj": nrm(ks[15], (DEPTH, GMLP_W, D_MODEL), GMLP_W ** -0.5),
        "w_out": nrm(ks[16], (DEPTH, D_MODEL, D_MODEL), D_MODEL ** -0.5),
        "w_ffn_in": nrm(ks[17], (DEPTH, D_MODEL, 2 * D_FF), D_MODEL ** -0.5),
        "w_ffn_out": nrm(ks[18], (DEPTH, D_FF, D_MODEL), D_FF ** -0.5),
    }


def reference(x_prompt, x_sample, cache_attn_k, cache_attn_v, norm_pre_mix, norm_post_mix, norm_pre_ffn,
              norm_post_ffn, w_in, attn_sinks, gmlp_ln_g, gmlp_ln_b, gmlp_w_s, gmlp_b_s, w_attn_proj,
              w_gmlp_proj, w_out, w_ffn_in, w_ffn_out):
    xp, xs = x_prompt, x_sample
    kp_l, vp_l, ks_l, vs_l, gs_l = [], [], [], [], []
    for l in range(DEPTH):
        w = (norm_pre_mix[l], norm_post_mix[l], norm_pre_ffn[l], norm_post_ffn[l], w_in[l], attn_sinks[l],
             gmlp_ln_g[l], gmlp_ln_b[l], gmlp_w_s[l], gmlp_b_s[l], w_attn_proj[l], w_gmlp_proj[l], w_out[l],
             w_ffn_in[l], w_ffn_out[l])
        xp, kp, vp, _ = trunk_layer(xp, *w, None, None)
        xs, ks, vs, gvs = trunk_layer(xs, *w, cache_attn_k[l], cache_attn_v[l])
        kp_l.append(kp); vp_l.append(vp); ks_l.append(ks); vs_l.append(vs); gs_l.append(gvs)
    return (xp, xs, jnp.stack(kp_l), jnp.stack(vp_l), jnp.stack(ks_l), jnp.stack(vs_l), jnp.stack(gs_l))
```

```python
import numpy as np
from contextlib import ExitStack
import concourse.bass as bass
import concourse.mybir as mybir
from concourse.bass_utils import run_bass_kernel_spmd

F32 = mybir.dt.float32
BF16 = mybir.dt.bfloat16
AF = mybir.ActivationFunctionType
ALU = mybir.AluOpType
AX = mybir.AxisListType

PE, ACT, DVE, POOL, SP = "pe", "act", "dve", "pool", "sp"
ENGS = [PE, ACT, DVE, POOL, SP]
EIDX = {e: i for i, e in enumerate(ENGS)}

D = 1024
SEQ = 4096
NCORES = 8
DFF = 2816
EPS = 1e-6


class Op:
    __slots__ = ("eng", "emit", "idx", "waits", "need_inc", "clock", "dma_sem", "dma_val", "tag")

    def __init__(self, eng, emit):
        self.eng = eng
        self.emit = emit
        self.waits = []
        self.need_inc = False
        self.dma_sem = None
        self.dma_val = 0


class MK:
    def __init__(self, nc):
        self.nc = nc
        self.ops = {e: [] for e in ENGS}
        self.last_w = {}
        self.readers = {}
        self.known = {e: [-1] * len(ENGS) for e in ENGS}
        self.known_dma = {e: {} for e in ENGS}
        self.dma_count = {}
        self.dma_keys = []
        self.pending = {e: [] for e in ENGS}
        self.stage = "setup"

    def _deps(self, reads, writes, eng=None):
        deps = []
        for k in reads:
            t = self.last_w.get(k)
            if t is not None:
                deps.append(t)
            if eng is not None and isinstance(k, tuple) and k[0] == "ps":
                for r in self.readers.get(k, ()):
                    if r[0] == "e" and r[1] != eng:
                        deps.append(r)
        for k in writes:
            t = self.last_w.get(k)
            if t is not None:
                deps.append(t)
            deps.extend(self.readers.get(k, ()))
        return deps

    def fence(self, engs, keys):
        deps = self._deps([], keys)
        for e in engs:
            self.pending[e].extend(deps)

    def _add(self, eng, emit, reads, writes, dma_sem=None):
        op = Op(eng, emit)
        op.tag = self.stage
        lst = self.ops[eng]
        op.idx = len(lst)
        deps = self._deps(reads, writes, eng)
        if self.pending[eng]:
            deps.extend(self.pending[eng])
            self.pending[eng] = []
        kn = self.known[eng]
        kd = self.known_dma[eng]
        best = {}
        for t in deps:
            k = (t[0], t[1])
            o = best.get(k)
            if o is None or t[2] > o[2]:
                best[k] = t
        for t in best.values():
            if t[0] == "e":
                _, de, di, dclk = t
                if de == eng and eng in (PE, SP):
                    continue
                j = EIDX[de]
                if kn[j] >= di:
                    continue
                if de == eng and op.idx - di >= 4:
                    continue
                op.waits.append(("e", de, di))
                self.ops[de][di].need_inc = True
                kn[j] = di
                for jj, v in enumerate(dclk):
                    if v > kn[jj]:
                        kn[jj] = v
            else:
                _, sk, val = t
                if kd.get(sk, 0) >= val:
                    continue
                assert val == self.dma_count[sk], ("ambiguous DMA semaphore wait", sk, val, self.dma_count[sk])
                op.waits.append(("d", sk, val))
                kd[sk] = val
        if dma_sem is not None:
            if dma_sem not in self.dma_count:
                self.dma_count[dma_sem] = 0
                self.dma_keys.append(dma_sem)
            self.dma_count[dma_sem] += 16
            op.dma_sem = dma_sem
            op.dma_val = self.dma_count[dma_sem]
            tok = ("d", dma_sem, op.dma_val)
        else:
            clk = list(kn)
            clk[EIDX[eng]] = op.idx
            tok = ("e", eng, op.idx, tuple(clk))
        for k in writes:
            self.last_w[k] = tok
            self.readers[k] = []
        for k in reads:
            if k not in writes:
                self.readers.setdefault(k, []).append(tok)
        lst.append(op)
        return op

    def op(self, eng, emit, reads=(), writes=()):
        return self._add(eng, emit, list(reads), list(writes))

    def dma(self, eng, sem, out, in_, reads=(), writes=()):
        def emit(e):
            return e.dma_start(out=out, in_=in_)
        return self._add(eng, emit, list(reads), list(writes), dma_sem=sem)

    def finish(self):
        nc = self.nc
        with ExitStack() as st:
            esem = {e: st.enter_context(nc.semaphore("s_" + e)) for e in ENGS}
            dsem = {k: st.enter_context(nc.semaphore("d_%d" % i)) for i, k in enumerate(self.dma_keys)}
            semval = {}
            for e in ENGS:
                c = 0
                vals = []
                for o in self.ops[e]:
                    if o.need_inc:
                        c += 1
                    vals.append(c)
                semval[e] = vals
            block = st.enter_context(nc.Block())

            def run(eng_name, e, final=False):
                for o in self.ops[eng_name]:
                    for w in o.waits:
                        if w[0] == "e":
                            e.wait_ge(esem[w[1]], semval[w[1]][w[2]])
                        else:
                            e.wait_ge(dsem[w[1]], w[2])
                    ins = o.emit(e)
                    if o.dma_sem is not None:
                        ins.then_inc(dsem[o.dma_sem], 16)
                    elif o.need_inc:
                        ins.then_inc(esem[eng_name], 1)
                if final:
                    for k in self.dma_keys:
                        e.wait_ge(dsem[k], self.dma_count[k])

            @block.tensor
            def _(e):
                run(PE, e)

            @block.scalar
            def _(e):
                run(ACT, e)

            @block.vector
            def _(e):
                run(DVE, e)

            @block.gpsimd
            def _(e):
                run(POOL, e)

            @block.sync
            def _(e):
                run(SP, e, final=True)


def qperm(h):
    hkv, g = divmod(h, 4)
    return (hkv // 2) * 4 + g, hkv % 2


STOP = [99]


def build_program(n_prompt_tiles=16, do_sample=True):
    nc = bass.Bass("TRN2", target_bir_lowering=False)
    dt_in = lambda name, shape: nc.dram_tensor(name, list(shape), F32, kind="ExternalInput").ap()
    dt_out = lambda name, shape: nc.dram_tensor(name, list(shape), F32, kind="ExternalOutput").ap()
    xp = dt_in("xp", [2, SEQ, D])
    xs = dt_in("xs", [32, D])
    ck = dt_in("ck", [2, 128, 256])
    cv = dt_in("cv", [2, 128, 256])
    g_in = [dt_in("g%d" % i, [1, D]) for i in range(4)]
    w_in = dt_in("w_in", [D, 5632])
    sinks = dt_in("sinks", [1, 16])
    ln_g = dt_in("ln_g", [1, D])
    ln_b = dt_in("ln_b", [1, D])
    w_s = dt_in("w_s", [4, 128, 128])
    b_s = dt_in("b_s", [1, 512])
    wap = dt_in("wap", [D, D])
    wgp = dt_in("wgp", [D, D])
    wout = dt_in("wout", [D, D])
    wfi = dt_in("wfi", [D, 5632])
    wfo = dt_in("wfo", [DFF, D])
    yp = dt_out("yp", [2, SEQ, D])
    ys = dt_out("ys", [32, D])
    kp = dt_out("kp", [2, 128, 256])
    vp = dt_out("vp", [2, 128, 256])
    ks = dt_out("ks", [2, 128, 256])
    vs = dt_out("vs", [2, 128, 256])
    gs = dt_out("gs", [32, D])
    sc = lambda name, shape: nc.dram_tensor(name, list(shape), BF16, kind="Internal").ap()
    s_win = sc("s_win", [128, 11, 8, 512])
    s_wap = sc("s_wap", [128, 2, 8, 512])
    s_wgp = sc("s_wgp", [128, 2, 8, 512])
    s_wout = sc("s_wout", [128, 2, 8, 512])
    s_wfi = sc("s_wfi", [128, 11, 8, 512])
    s_wfo = sc("s_wfo", [128, 2, 22, 512])

    with ExitStack() as st:
        sb = lambda name, shape, dt: st.enter_context(nc.sbuf_tensor(name, list(shape), dt))
        m = MK(nc)
        NRING = 8
        ring = sb("ring", [128, NRING, 8, 512], BF16)
        xt = sb("xt", [128, 6, D], F32)
        hT = sb("hT", [128, 8, 512], BF16)
        big = sb("big", [128, 24, 512], BF16)
        qT = big[:, 0:8, :]
        uT = big[:, 8:16, :]
        sgaT = big[:, 16:24, :]
        actT = big[:, 0:22, :]
        sgbT = sb("sgbT", [128, 8, 512], BF16)
        aT = sb("aT", [128, 8, 512], BF16)
        mgT = sb("mgT", [128, 8, 512], BF16)
        kT = sb("kT", [128, 2, 640], BF16)
        Vt = sb("Vt", [128, 5, 256], BF16)
        gvf = sb("gvf", [128, D], F32)
        gvn = sb("gvn", [128, 1, D], BF16)
        PT = sb("PT", [128, 8, 512], BF16)
        dent = sb("dent", [128, 2, 512], F32)
        junk = dent[:, 0, :].bitcast(BF16).rearrange("p (a b) -> p a b", a=2)
        hst = gvn
        tmpf = sb("tmpf", [128, 4, 512], F32)
        kvo = tmpf[:, 0, :]
        stt = sb("stt", [128, 64], F32)
        gS = sb("gS", [128, 4, D], F32)
        lng = sb("lng", [128, D], F32)
        lnb = sb("lnb", [128, D], F32)
        es = sb("es", [128, 16], F32)
        es2 = sb("es2", [128, 8], F32)
        negh = sb("negh", [128, 4], F32)
        ident = sb("ident", [128, 128], BF16)
        ones = sb("ones", [128, 128], BF16)
        WmT = sb("WmT", [128, 4, 128], BF16)
        WmT16 = sb("WmT16", [32, 4, 32], BF16)
        wsf = sb("wsf", [128, 128], F32)
        wsb = sb("wsb", [128, 128], BF16)
        bsb = sb("bsb", [1, 4, 128], BF16)
        bs16 = sb("bs16", [1, 4, 32], BF16)
        msk = sb("msk", [32, 2, 64], BF16)
        kTc = sb("kTc", [128, 2, 2, 128], BF16)
        Vc = sb("Vc", [128, 2, 256], BF16)
        ckf = sb("ckf", [128, 256], F32)
        ckb = sb("ckb", [128, 256], BF16)
        banks = [st.enter_context(nc.psum_tensor("ps%d" % i, [128, 512], F32)) for i in range(8)]
        bank_ctr = [0]

        kvst_ctr = [0]

        def kvsem():
            kvst_ctr[0] += 1
            return ("st_kv", kvst_ctr[0])

        def nb():
            i = 1 + bank_ctr[0] % 7
            bank_ctr[0] += 1
            return i

        def mm(b, out, lhsT, rhs, start, stop, reads):
            m.op(PE, lambda e: e.matmul(out, lhsT=lhsT, rhs=rhs, start=start, stop=stop), reads=reads,
                 writes=[("ps", b)])

        def act(out, in_, func, reads, writes, scale=1.0, accum_out=None):
            if accum_out is None:
                m.op(ACT, lambda e: e.activation(out=out, in_=in_, func=func, scale=scale), reads=reads, writes=writes)
            else:
                m.op(ACT, lambda e: e.activation(out=out, in_=in_, func=func, scale=scale, accum_out=accum_out),
                     reads=reads, writes=writes)

        def tcopy(eng, out, in_, reads, writes):
            if eng == ACT:
                m.op(eng, lambda e: e.activation(out=out, in_=in_, func=AF.Copy), reads=reads, writes=writes)
            else:
                m.op(eng, lambda e: e.tensor_copy(out=out, in_=in_), reads=reads, writes=writes)

        def tt(eng, out, in0, in1, op, reads, writes):
            m.op(eng, lambda e: e.tensor_tensor(out=out, in0=in0, in1=in1, op=op), reads=reads, writes=writes)

        def ts(eng, out, in0, s1, s2, op0, op1, reads, writes):
            if s2 is None:
                m.op(eng, lambda e: e.tensor_scalar(out=out, in0=in0, scalar1=s1, scalar2=None, op0=op0),
                     reads=reads, writes=writes)
            else:
                m.op(eng, lambda e: e.tensor_scalar(out=out, in0=in0, scalar1=s1, scalar2=s2, op0=op0, op1=op1),
                     reads=reads, writes=writes)

        def stt_op(eng, out, in0, scalar, in1, op0, op1, reads, writes):
            m.op(eng, lambda e: e.scalar_tensor_tensor(out=out, in0=in0, scalar=scalar, in1=in1, op0=op0, op1=op1),
                 reads=reads, writes=writes)

        def wsrc(w, c0, n):
            return w[:, c0:c0 + n].rearrange("(kc p) n -> p kc n", p=128)

        conv_jobs = {}
        blk_view = {}

        def cj(key, sem, idx, in_):
            conv_jobs.setdefault(key, []).append((sem, idx, in_))

        A_ = slice(None)
        for cb in range(11):
            blk_view[("s_win", cb)] = s_win[:, cb, :, :]
            blk_view[("s_wfi", cb)] = s_wfi[:, cb, :, :]
        for cb in range(2):
            blk_view[("s_wap", cb)] = s_wap[:, cb, :, :]
            blk_view[("s_wgp", cb)] = s_wgp[:, cb, :, :]
            blk_view[("s_wout", cb)] = s_wout[:, cb, :, :]
        for h in range(16):
            t, half = qperm(h)
            cb, j = divmod(t, 4)
            c0 = j * 128 + half * 64
            cj(("s_win", cb), ("cv_win", cb), (A_, A_, slice(c0, c0 + 64)), wsrc(w_in, h * 64, 64))
        for cb in range(2, 11):
            cj(("s_win", cb), ("cv_win", cb), (A_, A_, A_), wsrc(w_in, cb * 512, 512))
        for h in range(16):
            t, half = qperm(h)
            for cb in range(2):
                cj(("s_wap", cb), ("cv_wap", cb), (slice(half * 64, half * 64 + 64), t, A_),
                   wap[h * 64:(h + 1) * 64, cb * 512:(cb + 1) * 512])
        for cb in range(2):
            cj(("s_wgp", cb), ("cv_wgp", cb), (A_, A_, A_), wsrc(wgp, cb * 512, 512))
        for cb in range(2):
            cj(("s_wout", cb), ("cv_wout", cb), (A_, A_, A_), wsrc(wout, cb * 512, 512))
        for i in range(11):
            cj(("s_wfi", i), ("cv_wfi", i), (A_, A_, slice(0, 256)), wsrc(wfi, 256 * i, 256))
            cj(("s_wfi", i), ("cv_wfi", i), (A_, A_, slice(256, 512)), wsrc(wfi, DFF + 256 * i, 256))
        for n in range(2):
            for c, (k0, k1) in enumerate(((0, 8), (8, 16), (16, 22))):
                blk_view[("s_wfo", n, c)] = s_wfo[:, n, k0:k1, :]
                cj(("s_wfo", n, c), ("cv_wfo", n, c), (A_, A_, A_),
                   wfo[k0 * 128:k1 * 128, n * 512:(n + 1) * 512].rearrange("(kc p) n -> p kc n", p=128))
        conv_order = ([("s_win", c) for c in (0, 1, 2, 3, 4, 7, 5, 6, 8, 9, 10)]
                      + [("s_wap", 0), ("s_wgp", 0), ("s_wap", 1), ("s_wgp", 1), ("s_wout", 0), ("s_wout", 1)]
                      + [("s_wfi", i) for i in range(11)]
                      + [("s_wfo", n, c) for n in range(2) for c in range(3)])
        conv_ptr = [0]
        CONV_LOOKAHEAD = [14]

        def conv_upto(key):
            if conv_ptr[0] >= len(conv_order):
                return
            tgt = min(len(conv_order), conv_order.index(key) + 1 + CONV_LOOKAHEAD[0])
            while conv_ptr[0] < tgt:
                k = conv_order[conv_ptr[0]]
                for sem, idx, in_ in conv_jobs[k]:
                    m.dma(POOL, sem, blk_view[k][idx], in_, writes=[k])
                conv_ptr[0] += 1

        x_preloaded = [False]
        if n_prompt_tiles > 0:
            for b in range(4):
                m.dma(SP, ("xld", b), xt[:, b, :], xp[0, b * 128:(b + 1) * 128, :], writes=[("xt", b)])
            x_preloaded[0] = True

        for i in range(4):
            m.dma(SP, ("c_g", i), gS[:, i, :], g_in[i][0:1, :].partition_broadcast(128), writes=[("gS", i)])
            act(gS[:, i, :], gS[:, i, :], AF.Copy, [("gS", i)], [("gS", i)], scale=32.0)
        m.dma(SP, "c_lng", lng[:], ln_g[0:1, :].partition_broadcast(128), writes=["lng"])
        m.dma(SP, "c_lnb", lnb[:], ln_b[0:1, :].partition_broadcast(128), writes=["lnb"])
        m.dma(SP, "c_es", es[:], sinks[0:1, :].partition_broadcast(128), writes=["es"])
        act(es[:], es[:], AF.Exp, ["es"], ["es"])
        for tp_ in range(2):
            for hf in range(2):
                tcopy(DVE, es2[64 * hf:64 * hf + 64, 4 * tp_:4 * tp_ + 4],
                      es[64 * hf:64 * hf + 64, 4 * (2 * tp_ + hf):4 * (2 * tp_ + hf) + 4], ["es"], ["es2"])
        m.op(POOL, lambda e: e.memset(negh[:], -0.5), writes=["negh"])
        m.op(POOL, lambda e: e.memset(ones[:], 1.0), writes=["ones"])
        m.op(POOL, lambda e: e.memset(wsf[:], 1.0), writes=["wsf"])
        m.op(POOL, lambda e: e.affine_select(out=wsf[:], in_=wsf[:], pattern=[[-1, 128]],
                                             compare_op=ALU.is_equal, fill=0.0, base=0, channel_multiplier=1),
             reads=["wsf"], writes=["wsf"])
        tcopy(DVE, ident[:], wsf[:], ["wsf"], ["ident"])
        m.op(POOL, lambda e: e.memset(PT[:], 0.0), writes=[("PT", i) for i in range(8)])
        pT_b = 0
        pTv = banks[pT_b][:].bitcast(BF16)
        def setup_wmt():
            for g in range(4):
                m.dma(SP, "c_ws", wsf[:], w_s[g, :, :], writes=["wsf"])
                m.op(POOL, lambda e: e.affine_select(out=wsf[:], in_=wsf[:], pattern=[[-1, 128]], compare_op=ALU.is_ge,
                                                     fill=0.0, base=0, channel_multiplier=1),
                     reads=["wsf"], writes=["wsf"])
                tcopy(DVE, wsb[:], wsf[:], ["wsf"], ["wsb"])
                m.op(PE, lambda e: e.transpose(out=pTv[:, 0:128], in_=wsb[:], identity=ident[:]),
                     reads=["wsb", "ident"], writes=[("ps", pT_b)])
                tcopy(DVE, WmT[:, g, :], pTv[:, 0:128], [("ps", pT_b)], ["WmT"])
            m.dma(SP, "c_bs", kvo[0:1, :], b_s[0:1, :], writes=[("tmpf", 0)])
            tcopy(DVE, bsb[:].rearrange("p g t -> p (g t)"), kvo[0:1, :], [("tmpf", 0)], ["bsb"])

        def setup_sample_consts():
            if True:
                for g in range(4):
                    m.op(POOL, lambda e: e.memset(wsf[0:32, 0:32], 0.0), writes=["wsf"])
                    for s in range(2):
                        m.dma(SP, "c_ws", wsf[16 * s:16 * s + 16, 16 * s:16 * s + 16], w_s[g, 0:16, 0:16], reads=[],
                              writes=["wsf"])
                    m.op(POOL, lambda e: e.affine_select(out=wsf[0:32, 0:32], in_=wsf[0:32, 0:32], pattern=[[-1, 32]],
                                                         compare_op=ALU.is_ge, fill=0.0, base=0, channel_multiplier=1),
                         reads=["wsf"], writes=["wsf"])
                    tcopy(DVE, wsb[0:32, 0:32], wsf[0:32, 0:32], ["wsf"], ["wsb"])
                    m.op(PE, lambda e: e.transpose(out=pTv[0:32, 0:32], in_=wsb[0:32, 0:32], identity=ident[0:32, 0:32]),
                         reads=["wsb", "ident"], writes=[("ps", pT_b)])
                    tcopy(DVE, WmT16[:, g, :], pTv[0:32, 0:32], [("ps", pT_b)], ["WmT16"])
                for s in range(2):
                    tcopy(DVE, bs16[:, :, 16 * s:16 * s + 16], bsb[:, :, 0:16], ["bsb"], ["bs16"])
                m.op(POOL, lambda e: e.memset(msk[:], 1.0), writes=["msk"])
                m.op(POOL, lambda e: e.affine_select(out=msk[:, 0, :], in_=msk[:, 0, :], pattern=[[0, 64]],
                                                     compare_op=ALU.is_ge, fill=0.0, base=15, channel_multiplier=-1),
                     reads=["msk"], writes=["msk"])
                m.op(POOL, lambda e: e.affine_select(out=msk[:, 1, :], in_=msk[:, 1, :], pattern=[[0, 64]],
                                                     compare_op=ALU.is_ge, fill=0.0, base=-16, channel_multiplier=1),
                     reads=["msk"], writes=["msk"])


        ring_ctr = [0]

        FIRST_PASS = [False]

        def wload(src_ap, src_key, nkc=8):
            s = ring_ctr[0] % NRING
            ring_ctr[0] += 1
            if FIRST_PASS[0]:
                slot_v = ring[:, s, 0:nkc, :]
                for sem, idx, in_ in conv_jobs[src_key]:
                    m.dma(POOL, ("ring0", s), slot_v[idx], in_, writes=[("ring", s)])
                m.dma(SP, ("cvst", src_key), blk_view[src_key], slot_v, reads=[("ring", s)], writes=[src_key])
                return ring[:, s, :, :], ("ring", s)
            conv_upto(src_key)
            m.dma(SP, ("ring", s), ring[:, s, 0:nkc, :], src_ap, reads=[src_key], writes=[("ring", s)])
            return ring[:, s, :, :], ("ring", s)

        stat_ctr = [0]

        def stat(n=1):
            c = stat_ctr[0] % 64
            if c + n > 64:
                c = 0
            stat_ctr[0] = c + n
            return c

        tmp_ctr = [0]

        def ntmp():
            i = tmp_ctr[0] % 4
            tmp_ctr[0] += 1
            return i

        hst_ctr = [0]
        junk_ctr = [0]

        pre_done = [False]
        wmt_done = [False]

        PX = [POOL]
        YQ = [POOL]

        def rsqrt_op(P, out_ap, in_ap, rkeys, wkeys):
            if PX[0] == POOL:
                tt(POOL, out_ap, in_ap, negh[0:P, 0:1], ALU.pow, rkeys + ["negh"], wkeys)
            else:
                act(out_ap, in_ap, AF.Sqrt, rkeys, wkeys)
                m.op(DVE, lambda e: e.reciprocal(out=out_ap, in_=out_ap), reads=wkeys, writes=wkeys)

        def xkeys(xslot):
            return [("xt", xslot), ("xth", xslot, 0), ("xth", xslot, 1)]

        def norm_stats(P, xslot):
            c = stat(1)
            act(junk[0:P, :, :].rearrange("p a b -> p (a b)"), xt[0:P, xslot, :], AF.Square,
                xkeys(xslot), [("st", c), ("dent", 0)], accum_out=stt[0:P, c:c + 1])
            c2 = stat(2)
            ts(DVE, stt[0:P, c2:c2 + 1], stt[0:P, c:c + 1], float(D * EPS), None, ALU.add, None,
               [("st", c)], [("st", c2)])
            rsqrt_op(P, stt[0:P, c2 + 1:c2 + 2], stt[0:P, c2:c2 + 1], [("st", c2)], [("st", c2 + 1)])
            return c2 + 1

        HKEYS = [("gvn", 0, 0), ("gvn", 0, 1)]

        def norm_scale(P, gi, xslot, r):
            stt_op(DVE, hst[0:P, 0, :], xt[0:P, xslot, :], stt[0:P, r:r + 1], gS[0:P, gi, :], ALU.mult, ALU.mult,
                   xkeys(xslot) + [("st", r), ("gS", gi)], HKEYS)

        def norm_T(P, b):
            for kc in range(8):
                m.op(PE, (lambda kc: lambda e: e.transpose(out=pTv[:, kc * P:(kc + 1) * P],
                                                           in_=hst[0:P, 0, kc * 128:(kc + 1) * 128],
                                                           identity=ident[0:P, 0:P]))(kc),
                     reads=HKEYS + ["ident"], writes=[("ps", pT_b)])
            act(hT[:, :, b * P:(b + 1) * P], pTv[:, 0:8 * P].rearrange("p (k t) -> p k t", k=8), AF.Copy,
                [("ps", pT_b)], [("hT", b)])

        def norm_apply(P, b, gi, xslot, r):
            norm_scale(P, gi, xslot, r)
            norm_T(P, b)

        def norm_transpose(P, b, gi, xslot):
            norm_apply(P, b, gi, xslot, norm_stats(P, xslot))

        def xsrc(sample, seq, t0, b):
            return xs[:, :] if sample else xp[seq, t0 + b * 128:t0 + (b + 1) * 128, :]

        def run_pass(sample, seq=0, t0=0, xslots=(0,), nxt=None):
            P = 32 if sample else 128
            NB = 1 if sample else 4
            N = P * NB
            first_block_of_seq = (t0 == 0)
            last_tile_of_seq = (t0 + 512 == SEQ)
            hkeys = [("hT", b) for b in range(NB)]

            m.stage = "norm1"
            if not pre_done[0]:
                if x_preloaded[0] and not sample:
                    x_preloaded[0] = False
                else:
                    for b in range(NB):
                        m.dma(SP, ("xld", xslots[b]), xt[0:P, xslots[b], :], xsrc(sample, seq, t0, b),
                              writes=[("xt", xslots[b])])
                for b in range(NB):
                    norm_transpose(P, b, 0, xslots[b])
            pre_done[0] = False

            m.stage = "A.q"

            loop_banks = [None]
            lb_ctr = [0]
            S_BANKS = (1, 2, 6)
            sb_ctr = [0]

            def nbl():
                if loop_banks[0] is None:
                    return nb()
                i = loop_banks[0][lb_ctr[0] % 2]
                lb_ctr[0] += 1
                return i

            def ws_tile(wv, wkey, c0, dst, dkey, func, eng_copy=None, scale=1.0):
                b_ = nbl()
                for kc in range(8):
                    mm(b_, banks[b_][:, 0:N], wv[:, kc, c0:c0 + 128], hT[:, kc, 0:N], kc == 0, kc == 7,
                       [wkey] + hkeys)
                if eng_copy is not None:
                    tcopy(eng_copy, dst, banks[b_][:, 0:N], [("ps", b_)], [dkey])
                else:
                    act(dst, banks[b_][:, 0:N], func, [("ps", b_)], [dkey], scale=scale)

            for cb in range(2):
                wv, wk = wload(s_win[:, cb, :, :], ("s_win", cb))
                for j in range(4):
                    ws_tile(wv, wk, 128 * j, qT[:, 4 * cb + j, 0:N], ("qT", 4 * cb + j), AF.Copy)
            m.stage = "A.kv"
            wv, wk = wload(s_win[:, 2, :, :], ("s_win", 2))
            for j in range(2):
                if sample:
                    ws_tile(wv, wk, 128 * j, kT[:, j, 0:32], ("kT", 0), None, eng_copy=DVE)
                else:
                    ws_tile(wv, wk, 128 * j, kT[:, j, 128:640], ("kTw", j), None, eng_copy=DVE)
            if not sample:
                for sl in range(1, 5):
                    m.last_w[("kT", sl)] = m.last_w[("kTw", 1)]
                    m.readers[("kT", sl)] = []
            for b in range(NB):
                b_ = nb()
                need_k = (not sample) and last_tile_of_seq and b == NB - 1
                c0 = 0 if need_k else 256
                for kc in range(8):
                    mm(b_, banks[b_][0:P, c0:512], hT[:, kc, b * P:(b + 1) * P], wv[:, kc, c0:512], kc == 0, kc == 7,
                       [wk, ("hT", b)])
                slot = 0 if sample else b + 1
                tcopy(DVE, Vt[0:P, slot, :], banks[b_][0:P, 256:512], [("ps", b_)], [("Vt", slot)])
                if sample:
                    for s in range(2):
                        bq = nb()
                        for kc in range(8):
                            mm(bq, banks[bq][0:16, :], hT[:, kc, 16 * s:16 * s + 16], wv[:, kc, :], kc == 0, kc == 7,
                               [wk, ("hT", b)])
                        tcopy(ACT, kvo[0:16, :], banks[bq][0:16, :], [("ps", bq)], [("tmpf", 0)])
                        m.dma(POOL, kvsem(), ks[s, 112:128, :], kvo[0:16, 0:256], reads=[("tmpf", 0)])
                        m.dma(POOL, kvsem(), vs[s, 112:128, :], kvo[0:16, 256:512], reads=[("tmpf", 0)])
                        for src_c, dst_c in ((ck, ks), (cv, vs)):
                            m.dma(SP, "c_ck", ckf[0:112, :], src_c[s, 16:128, :], writes=["ckf"])
                            m.dma(POOL, kvsem(), dst_c[s, 0:112, :], ckf[0:112, :], reads=["ckf"])
                elif last_tile_of_seq and b == NB - 1:
                    tcopy(ACT, kvo[:, :], banks[b_][:, :], [("ps", b_)], [("tmpf", 0)])
                    m.dma(POOL, kvsem(), kp[seq, :, :], kvo[:, 0:256], reads=[("tmpf", 0)])
                    m.dma(POOL, kvsem(), vp[seq, :, :], kvo[:, 256:512], reads=[("tmpf", 0)])
            m.stage = "A.u"
            for cb in (3, 4):
                wv, wk = wload(s_win[:, cb, :, :], ("s_win", cb))
                for j in range(4):
                    f = 4 * (cb - 3) + j
                    ws_tile(wv, wk, 128 * j, uT[:, f, 0:N], ("uT", f), AF.Gelu_apprx_tanh)
            if not wmt_done[0]:
                m.stage = "setup"
                setup_wmt()
                wmt_done[0] = True
            ukeys = [("uT", j) for j in range(8)]
            qkeys = [("qT", j) for j in range(8)]
            akeys = [("aT", j) for j in range(8)]

            gvw = {}

            def gv_mm(b):
                m.stage = "A.gv"
                if not gvw:
                    gvw[5] = wload(s_win[:, 5, :, :], ("s_win", 5))
                    gvw[6] = wload(s_win[:, 6, :, :], ("s_win", 6))
                for hh in range(2):
                    wv, wk = gvw[5 + hh]
                    b_ = nbl()
                    for kc in range(8):
                        mm(b_, banks[b_][0:P, :], hT[:, kc, b * P:(b + 1) * P], wv[:, kc, :], kc == 0, kc == 7,
                           [wk, ("hT", b)])
                    act(gvf[0:P, hh * 512:(hh + 1) * 512], banks[b_][0:P, :], AF.Gelu_apprx_tanh, [("ps", b_)],
                        [("gvf", hh)])
                c = stat(16)
                for hh in range(2):
                    m.op(DVE, (lambda hh, c: lambda e: e.bn_stats(out=stt[0:P, c + 6 * hh:c + 6 * hh + 6],
                                                                  in_=gvf[0:P, hh * 512:(hh + 1) * 512]))(hh, c),
                         reads=[("gvf", hh)], writes=[("st", c + 6 * hh)])
                m.op(DVE, (lambda c: lambda e: e.bn_aggr(out=stt[0:P, c + 12:c + 14], in_=stt[0:P, c:c + 12]))(c),
                     reads=[("st", c), ("st", c + 6)], writes=[("st", c + 12)])
                ts(DVE, stt[0:P, c + 14:c + 15], stt[0:P, c + 13:c + 14], float(EPS), None, ALU.add, None,
                   [("st", c + 12)], [("st", c + 14)])
                rsqrt_op(P, stt[0:P, c + 15:c + 16], stt[0:P, c + 14:c + 15], [("st", c + 14)], [("st", c + 15)])
                gk = [("gvf", 0), ("gvf", 1)]
                gslot = 0
                for hh, eng in ((0, DVE), (1, PX[0])):
                    cs = slice(hh * 512, (hh + 1) * 512)
                    gkh = [("gvf", hh)]
                    ts(DVE, gvf[0:P, cs], gvf[0:P, cs], stt[0:P, c + 12:c + 13], stt[0:P, c + 15:c + 16],
                       ALU.subtract, ALU.mult, gkh + [("st", c + 12), ("st", c + 15)], gkh)
                    tt(eng, gvf[0:P, cs], gvf[0:P, cs], lng[0:P, cs], ALU.mult, gkh + ["lng"], gkh)
                    if sample:
                        tt(eng, gvf[0:P, cs], gvf[0:P, cs], lnb[0:P, cs], ALU.add, gkh + ["lnb"], gkh)
                    else:
                        tt(eng, gvn[0:P, gslot, cs], gvf[0:P, cs], lnb[0:P, cs], ALU.add, gkh + ["lnb"],
                           [("gvn", gslot, hh)])
                if sample:
                    m.dma(POOL, "st_gs", gs[:, :], gvf[0:P, :], reads=gk)
                    for hh in range(2):
                        cs = slice(hh * 512, (hh + 1) * 512)
                        tcopy(DVE, gvn[0:P, gslot, cs], gvf[0:P, cs], [("gvf", hh)], [("gvn", gslot, hh)])

            def spatial(b):
                m.stage = "A.sp"
                gslot = 0
                wmt = WmT16 if sample else WmT
                bsx = bs16 if sample else bsb
                for half in range(1 if sample else 2):
                    b_ = nbl()
                    jr = range(8) if sample else range(4 * half, 4 * half + 4)
                    for jj, j in enumerate(jr):
                        g = j // 2
                        o = banks[b_][:, jj * P:(jj + 1) * P]
                        mm(b_, o, gvn[0:P, gslot, j * 128:(j + 1) * 128], wmt[0:P, g, :], True, False,
                           [("gvn", gslot, j // 4), "WmT", "WmT16"])
                        mm(b_, o, ones[0:1, 0:128], bsx[0:1, g, :], False, True, ["ones", "bsb", "bs16"])
                    nj = len(jr)
                    j0 = jr[0]
                    tt(DVE, uT[:, j0:j0 + nj, b * P:(b + 1) * P],
                       banks[b_][:, 0:nj * P].rearrange("p (j t) -> p j t", j=nj),
                       uT[:, j0:j0 + nj, b * P:(b + 1) * P], ALU.mult,
                       [("ps", b_)] + ukeys[j0:j0 + nj], ukeys[j0:j0 + nj])

            gw = {}

            def gate_tile(n):
                m.stage = "A.gates"
                cb = 7 + n // 4
                if cb not in gw:
                    gw.clear()
                    gw[cb] = wload(s_win[:, cb, :, :], ("s_win", cb))
                wv, wk = gw[cb]
                if n < 8:
                    ws_tile(wv, wk, 128 * (n % 4), sgaT[:, n, 0:N], ("sga", n), AF.Tanh, scale=0.5)
                else:
                    ws_tile(wv, wk, 128 * (n % 4), sgbT[:, n - 8, 0:N], ("sgb", n - 8), AF.Tanh, scale=0.5)

            if sample:
                m.stage = "attn"
                for s in range(2):
                    m.dma(SP, "c_ck", ckf[:], ck[s, :, :], writes=["ckf"])
                    tcopy(DVE, ckb[:], ckf[:], ["ckf"], ["ckb"])
                    for j in range(2):
                        m.op(PE, (lambda j: lambda e: e.transpose(out=pTv[:, j * 128:(j + 1) * 128],
                                                                  in_=ckb[:, j * 128:(j + 1) * 128],
                                                                  identity=ident[:]))(j),
                             reads=["ckb", "ident"], writes=[("ps", pT_b)])
                    tcopy(DVE, kTc[:, s, :, :], pTv[:, 0:256].rearrange("p (j t) -> p j t", j=2), [("ps", pT_b)],
                          [("kTc", s)])
                    m.dma(SP, "c_ck", ckf[:], cv[s, :, :], writes=["ckf"])
                    tcopy(DVE, Vc[:, s, :], ckf[:], ["ckf"], [("Vc", s)])

            def attn_S(n, b, tp, half):
                m.stage = "attn"
                hs0 = half * 64
                pp = (n % 2) * 4 + half * 2
                first = (not sample) and first_block_of_seq and b == 0
                b1 = None
                if not first:
                    b1 = S_BANKS[sb_ctr[0] % 3]
                    sb_ctr[0] += 1
                b2 = S_BANKS[sb_ctr[0] % 3]
                sb_ctr[0] += 1
                if sample:
                    s_ = b
                    qv = qT[hs0:hs0 + 64, 4 * tp:4 * tp + 4, 16 * s_:16 * s_ + 16]
                    o1 = banks[b1][:, 0:64].rearrange("p (g q) -> p g q", g=4)
                    o2 = banks[b2][0:32, 0:64].rearrange("p (g q) -> p g q", g=4)
                    mm(b1, o1, kTc[hs0:hs0 + 64, s_, tp, :], qv, True, True, [("kTc", s_)] + qkeys)
                    mm(b2, o2, kT[hs0:hs0 + 64, tp, 0:32], qv, True, True, [("kT", 0)] + qkeys)
                    P1 = PT[:, pp, 0:64]
                    P2 = PT[0:32, pp + 1, 0:64]
                    act(P1, banks[b1][:, 0:64], AF.Exp, [("ps", b1)], [("PT", pp)], scale=0.125)
                    act(P2, banks[b2][0:32, 0:64], AF.Exp, [("ps", b2)], [("PT", pp + 1)], scale=0.125)
                    tt(DVE, P2, P2, msk[:, s_, :], ALU.mult, [("PT", pp + 1), "msk"], [("PT", pp + 1)])
                    return
                qv = qT[hs0:hs0 + 64, 4 * tp:4 * tp + 4, b * 128:(b + 1) * 128]
                P1 = PT[:, pp, :].rearrange("p (g q) -> p g q", g=4)
                P2 = PT[:, pp + 1, :].rearrange("p (g q) -> p g q", g=4)
                o1 = banks[b1][:, :].rearrange("p (g q) -> p g q", g=4) if b1 is not None else None
                o2 = banks[b2][:, :].rearrange("p (g q) -> p g q", g=4)
                if not first:
                    mm(b1, o1, kT[hs0:hs0 + 64, tp, b * 128:(b + 1) * 128], qv, True, True, [("kT", b)] + qkeys)
                mm(b2, o2, kT[hs0:hs0 + 64, tp, (b + 1) * 128:(b + 2) * 128], qv, True, True, [("kT", b + 1)] + qkeys)
                if not first:
                    act(P1[0:64, :, 0:64], o1[0:64, :, 0:64], AF.Exp, [("ps", b1)], [("PT", pp)], scale=0.125)
                    act(P1[64:128, :, :], o1[64:128, :, :], AF.Exp, [("ps", b1)], [("PT", pp)], scale=0.125)
                act(P2[0:64, :, :], o2[0:64, :, :], AF.Exp, [("ps", b2)], [("PT", pp + 1)], scale=0.125)
                act(P2[64:128, :, 64:128], o2[64:128, :, 64:128], AF.Exp, [("ps", b2)], [("PT", pp + 1)], scale=0.125)

            def attn_PV(n, b, tp):
                m.stage = "attn"
                d_i = n % 2
                bo, bd = (3 if n % 2 == 0 else 4), 5
                W = 64 if sample else 512
                NQ = 16 if sample else 128
                if sample:
                    for half in range(2):
                        hs0 = half * 64
                        pp = (n % 2) * 4 + half * 2
                        vc0 = 128 * tp + 64 * half
                        oo = banks[bo][hs0:hs0 + 64, 0:W]
                        od = banks[bd][hs0:hs0 + 64, 0:W]
                        s_ = b
                        P1 = PT[:, pp, 0:64]
                        P2 = PT[0:32, pp + 1, 0:64]
                        mm(bo, oo, Vc[:, s_, vc0:vc0 + 64], P1, True, False, [("Vc", s_), ("PT", pp)])
                        mm(bo, oo, Vt[0:32, 0, vc0:vc0 + 64], P2, False, True, [("Vt", 0), ("PT", pp + 1)])
                        mm(bd, od, ones[:, 0:64], P1, True, False, ["ones", ("PT", pp)])
                        mm(bd, od, ones[0:32, 0:64], P2, False, True, ["ones", ("PT", pp + 1)])
                else:
                    first = first_block_of_seq and b == 0
                    for bank_i, use_v in ((bo, True), (bd, False)):
                        for which in ((1, 2) if not first else (2,)):
                            for half in range(2):
                                hs0 = half * 64
                                pp = (n % 2) * 4 + half * 2 + (which - 1)
                                vc0 = 128 * tp + 64 * half
                                slot = b if which == 1 else b + 1
                                lhs = Vt[:, slot, vc0:vc0 + 64] if use_v else ones[:, 0:64]
                                rk = [("Vt", slot), ("PT", pp)] if use_v else ["ones", ("PT", pp)]
                                mm(bank_i, banks[bank_i][hs0:hs0 + 64, :], lhs, PT[:, pp, :],
                                   which == 1 or first, which == 2, rk)
                if sample:
                    a_out = aT[:, 4 * tp:4 * tp + 4, 16 * b:16 * b + 16]
                else:
                    a_out = aT[:, 4 * tp:4 * tp + 4, b * 128:(b + 1) * 128]
                dv = dent[:, d_i, 0:W]
                esb = es2[:, 4 * tp:4 * tp + 4].unsqueeze(2).to_broadcast([128, 4, NQ])
                tt(DVE, dv.rearrange("p (g q) -> p g q", g=4),
                   banks[bd][:, 0:W].rearrange("p (g q) -> p g q", g=4), esb, ALU.add,
                   [("ps", bd), "es2"], [("dent", d_i)])
                m.op(DVE, (lambda dv: lambda e: e.reciprocal(out=dv, in_=dv))(dv), reads=[("dent", d_i)],
                     writes=[("dent", d_i)])
                tt(DVE, a_out, banks[bo][:, 0:W].rearrange("p (g q) -> p g q", g=4),
                   dv.rearrange("p (g q) -> p g q", g=4), ALU.mult, [("ps", bo), ("dent", d_i)],
                   akeys[4 * tp:4 * tp + 4])

            pairs = [(b, tp) for b in range(2 if sample else NB) for tp in range(2)]
            NG = len(pairs)
            gpg = 16 // NG
            loop_banks[0] = (7, 0)
            gate_tile(0)
            attn_S(0, *pairs[0], 0)
            attn_S(0, *pairs[0], 1)
            for n, (b, tp) in enumerate(pairs):
                if n + 1 < NG:
                    attn_S(n + 1, *pairs[n + 1], 0)
                attn_PV(n, b, tp)
                if n + 1 < NG:
                    attn_S(n + 1, *pairs[n + 1], 1)
                for t in range(gpg):
                    gi_ = gpg * n + t + 1
                    if gi_ < 16:
                        gate_tile(gi_)
                if sample:
                    if n == 2:
                        spatial(0)
                    if n == 0:
                        gv_mm(0)
                else:
                    if n % 2 == 0 and n >= 2:
                        spatial(n // 2 - 1)
                    if n % 2 == 0:
                        gv_mm(n // 2)
            if not sample:
                spatial(NB - 1)
            loop_banks[0] = None
            if not sample:
                m.stage = "attn"
                tcopy(PX[0], kT[:, :, 0:128], kT[:, :, 512:640], [("kT", 4)], [("kT", 0)])
                tcopy(PX[0], Vt[:, 0, :], Vt[:, 4, :], [("Vt", 4)], [("Vt", 0)])
                m.fence([DVE], [("kT", sl) for sl in range(1, 5)])

            m.stage = "merge"
            for cbm in range(2):
                wva, wka = wload(s_wap[:, cbm, :, :], ("s_wap", cbm))
                wvg, wkg = wload(s_wgp[:, cbm, :, :], ("s_wgp", cbm))
                for j in range(4):
                    f = 4 * cbm + j
                    ba, bb = nb(), nb()
                    for kc in range(8):
                        mm(ba, banks[ba][:, 0:N], wva[:, kc, j * 128:(j + 1) * 128], aT[:, kc, 0:N], kc == 0, kc == 7,
                           [wka] + akeys)
                    for kc in range(8):
                        mm(bb, banks[bb][:, 0:N], wvg[:, kc, j * 128:(j + 1) * 128], uT[:, kc, 0:N], kc == 0, kc == 7,
                           [wkg] + ukeys)
                    t1, t2 = ntmp(), ntmp()
                    stt_op(DVE, tmpf[:, t1, 0:N], sgaT[:, f, 0:N], 1.0, banks[ba][:, 0:N], ALU.add, ALU.mult,
                           [("ps", ba), ("sga", f)], [("tmpf", t1)])
                    stt_op(DVE, tmpf[:, t2, 0:N], sgbT[:, f, 0:N], 1.0, banks[bb][:, 0:N], ALU.add, ALU.mult,
                           [("ps", bb), ("sgb", f)], [("tmpf", t2)])
                    tt(DVE, mgT[:, f, 0:N], tmpf[:, t1, 0:N], tmpf[:, t2, 0:N], ALU.add,
                       [("tmpf", t1), ("tmpf", t2)], [("mgT", f)])
            mkeys = [("mgT", f) for f in range(8)]

            def post_norm_residual(b, bks, gi, xslot, eps_mul=1.0, store_dst=None):
                c = stat(2)
                for n in range(2):
                    jk = junk_ctr[0] % 2
                    junk_ctr[0] += 1
                    act(junk[0:P, jk, :], banks[bks[n]][0:P, :], AF.Square, [("ps", bks[n])],
                        [("st", c + n), ("dent", 0)], accum_out=stt[0:P, c + n:c + n + 1])
                c2 = stat(2)
                ts(DVE, stt[0:P, c2:c2 + 1], stt[0:P, c:c + 1], stt[0:P, c + 1:c + 2], float(D * EPS * eps_mul),
                   ALU.add, ALU.add, [("st", c), ("st", c + 1)], [("st", c2)])
                rsqrt_op(P, stt[0:P, c2 + 1:c2 + 2], stt[0:P, c2:c2 + 1], [("st", c2)], [("st", c2 + 1)])
                for n in range(2):
                    t1 = ntmp()
                    stt_op(DVE, tmpf[0:P, t1, :], banks[bks[n]][0:P, :], stt[0:P, c2 + 1:c2 + 2],
                           gS[0:P, gi, n * 512:(n + 1) * 512], ALU.mult, ALU.mult,
                           [("ps", bks[n]), ("st", c2 + 1), ("gS", gi)], [("tmpf", t1)])
                    if store_dst is None:
                        tt(PX[0] if n == 0 else DVE, xt[0:P, xslot, n * 512:(n + 1) * 512],
                           xt[0:P, xslot, n * 512:(n + 1) * 512], tmpf[0:P, t1, :], ALU.add,
                           [("tmpf", t1), ("xt", xslot), ("xth", xslot, n)], [("xth", xslot, n)])
                    else:
                        tt(PX[0] if n == 0 else DVE, tmpf[0:P, t1, :], xt[0:P, xslot, n * 512:(n + 1) * 512],
                           tmpf[0:P, t1, :], ALU.add, [("tmpf", t1), ("xt", xslot), ("xth", xslot, n)],
                           [("tmpf", t1)])
                        m.dma(YQ[0], ("yst", t1, YQ[0]), store_dst[:, n * 512:(n + 1) * 512], tmpf[0:P, t1, :],
                              reads=[("tmpf", t1)])

            m.stage = "wout"
            wvo = [wload(s_wout[:, n, :, :], ("s_wout", n)) for n in range(2)]
            wbks = []
            r3 = {}

            def pn(b):
                m.stage = "wout"
                post_norm_residual(b, wbks[b], 1, xslots[b], eps_mul=4.0)

            def s3(b):
                m.stage = "norm3"
                r3[b] = norm_stats(P, xslots[b])

            def a3(b):
                m.stage = "norm3"
                norm_apply(P, b, 2, xslots[b], r3[b])

            def wout_mm(b):
                m.stage = "wout"
                bks = [nb(), nb()]
                wbks.append(bks)
                for n in range(2):
                    for kc in range(8):
                        mm(bks[n], banks[bks[n]][0:P, :], mgT[:, kc, b * P:(b + 1) * P], wvo[n][0][:, kc, :], kc == 0,
                           kc == 7, [wvo[n][1]] + mkeys)

            if NB == 1:
                wout_mm(0); pn(0); s3(0); a3(0)
            else:
                def a3s(b):
                    m.stage = "norm3"
                    norm_scale(P, 2, xslots[b], r3[b])

                def a3t(b):
                    m.stage = "norm3"
                    norm_T(P, b)

                wout_mm(0); wout_mm(1); pn(0)
                wout_mm(2); pn(1); s3(0); pn(2)
                wout_mm(3); a3s(0); s3(1); pn(3)
                a3t(0); a3s(1); s3(2)
                a3t(1); a3s(2); s3(3)
                a3t(2); a3s(3)
                a3t(3)

            m.stage = "ffn_in"
            if nxt is not None:
                for bb_ in range(2):
                    m.dma(SP, ("xld", nxt["xslots"][bb_]), xt[:, nxt["xslots"][bb_], :],
                          xsrc(False, nxt["seq"], nxt["t0"], bb_), writes=[("xt", nxt["xslots"][bb_])])
            m.fence([DVE], qkeys + ukeys + [("sga", j) for j in range(8)])
            for i in range(11):
                wv, wk = wload(s_wfi[:, i, :, :], ("s_wfi", i))
                for jj in range(2):
                    bg, bu = nb(), nb()
                    for kc in range(8):
                        mm(bg, banks[bg][:, 0:N], wv[:, kc, jj * 128:(jj + 1) * 128], hT[:, kc, 0:N], kc == 0, kc == 7,
                           [wk] + hkeys)
                    for kc in range(8):
                        mm(bu, banks[bu][:, 0:N], wv[:, kc, 256 + jj * 128:256 + (jj + 1) * 128], hT[:, kc, 0:N],
                           kc == 0, kc == 7, [wk] + hkeys)
                    t1 = ntmp()
                    act(tmpf[:, t1, 0:N], banks[bg][:, 0:N], AF.Silu, [("ps", bg)], [("tmpf", t1)])
                    tt(DVE, actT[:, 2 * i + jj, 0:N], banks[bu][:, 0:N], tmpf[:, t1, 0:N], ALU.mult,
                       [("ps", bu), ("tmpf", t1)], [("actT", 2 * i + jj)])
            fkeys = [("actT", j) for j in range(22)]

            m.stage = "ffn_out"
            chunks = ((0, 8), (8, 16), (16, 22))
            wfo_v = [[wload(s_wfo[:, n, k0:k1, :], ("s_wfo", n, c), nkc=k1 - k0) for c, (k0, k1) in enumerate(chunks)]
                     for n in range(2)]
            rn = {}
            for b in range(NB):
                m.stage = "ffn_out"
                if nxt is not None and 1 <= b <= 2:
                    m.dma(SP, ("xld", nxt["xslots"][b + 1]), xt[:, nxt["xslots"][b + 1], :],
                          xsrc(False, nxt["seq"], nxt["t0"], b + 1), writes=[("xt", nxt["xslots"][b + 1])])
                bks = [nb(), nb()]
                for n in range(2):
                    for c, (k0, k1) in enumerate(chunks):
                        wv, wk = wfo_v[n][c]
                        for kc in range(k0, k1):
                            mm(bks[n], banks[bks[n]][0:P, :], actT[:, kc, b * P:(b + 1) * P], wv[:, kc - k0, :],
                               kc == 0, kc == 21, [wk] + fkeys)
                if nxt is not None:
                    m.stage = "norm1"
                    if b == 0:
                        rn[0] = norm_stats(128, nxt["xslots"][0])
                        rn[1] = norm_stats(128, nxt["xslots"][1])
                    norm_scale(128, 0, nxt["xslots"][b], rn[b])
                    norm_T(128, b)
                    if 1 <= b <= 2:
                        rn[b + 1] = norm_stats(128, nxt["xslots"][b + 1])
                    m.stage = "ffn_out"
                dst = ys[:, :] if sample else yp[seq, t0 + b * 128:t0 + (b + 1) * 128, :]
                post_norm_residual(b, bks, 3, xslots[b], store_dst=dst)
            m.fence([ACT, DVE], fkeys)
            if nxt is not None:
                pre_done[0] = True

        tiles = [dict(seq=i // 8, t0=(i % 8) * 512, xslots=[(4 * i + b) % 6 for b in range(4)])
                 for i in range(n_prompt_tiles)]
        for i, t in enumerate(tiles):
            nxt = tiles[i + 1] if i + 1 < len(tiles) else None
            if i == 0:
                PX[0], YQ[0], FIRST_PASS[0] = DVE, ACT, True
            run_pass(False, seq=t["seq"], t0=t["t0"], xslots=t["xslots"], nxt=nxt)
            if i == 0:
                conv_ptr[0] = len(conv_order)
            PX[0], YQ[0], FIRST_PASS[0] = POOL, POOL, False
        if do_sample:
            m.stage = "setup"
            setup_sample_consts()
            run_pass(True, xslots=(0,))
        m.finish()
        nc._mk_tags = {e: [o.tag for o in m.ops[e]] for e in ENGS}
    return nc


_NC_CACHE = {}


def _get_nc():
    if "nc" not in _NC_CACHE:
        _NC_CACHE["nc"] = build_program()
    return _NC_CACHE["nc"]


def make_in_maps(inputs):
    f = lambda a: np.ascontiguousarray(np.asarray(a, dtype=np.float32))
    x_prompt = f(inputs["x_prompt"])
    x_sample = f(inputs["x_sample"])
    ck = f(inputs["cache_attn_k"])[0].reshape(16, 128, 256)
    cv = f(inputs["cache_attn_v"])[0].reshape(16, 128, 256)
    shared = {
        "g0": f(inputs["norm_pre_mix"]).reshape(1, D),
        "g1": f(inputs["norm_post_mix"]).reshape(1, D),
        "g2": f(inputs["norm_pre_ffn"]).reshape(1, D),
        "g3": f(inputs["norm_post_ffn"]).reshape(1, D),
        "w_in": f(inputs["w_in"])[0],
        "sinks": f(inputs["attn_sinks"]).reshape(1, 16),
        "ln_g": f(inputs["gmlp_ln_g"]).reshape(1, D),
        "ln_b": f(inputs["gmlp_ln_b"]).reshape(1, D),
        "w_s": f(inputs["gmlp_w_s"])[0],
        "b_s": f(inputs["gmlp_b_s"]).reshape(1, 512),
        "wap": f(inputs["w_attn_proj"])[0],
        "wgp": f(inputs["w_gmlp_proj"])[0],
        "wout": f(inputs["w_out"])[0],
        "wfi": f(inputs["w_ffn_in"])[0],
        "wfo": f(inputs["w_ffn_out"])[0],
    }
    in_maps = []
    for c in range(NCORES):
        d = dict(shared)
        d["xp"] = x_prompt[2 * c:2 * c + 2]
        d["xs"] = x_sample[2 * c:2 * c + 2].reshape(32, D)
        d["ck"] = ck[2 * c:2 * c + 2]
        d["cv"] = cv[2 * c:2 * c + 2]
        in_maps.append(d)
    return in_maps


def gather(results):
    cat = lambda k: np.concatenate([np.asarray(r[k]) for r in results], axis=0)
    y_prompt = cat("yp").reshape(16, SEQ, D).astype(np.float32)
    y_sample = cat("ys").reshape(16, 16, D).astype(np.float32)
    kp = cat("kp").reshape(1, 16, 128, 4, 64).astype(np.float32)
    vp = cat("vp").reshape(1, 16, 128, 4, 64).astype(np.float32)
    ks = cat("ks").reshape(1, 16, 128, 4, 64).astype(np.float32)
    vs = cat("vs").reshape(1, 16, 128, 4, 64).astype(np.float32)
    gs = cat("gs").reshape(1, 16, 16, D).astype(np.float32)
    return (y_prompt, y_sample, kp, vp, ks, vs, gs)


def kernel(**inputs):
    nc = _get_nc()
    in_maps = make_in_maps(inputs)
    res = run_bass_kernel_spmd(nc, in_maps, core_ids=list(range(NCORES)))
    return gather(res.results)
```

```python
import numpy as np
from contextlib import ExitStack
import concourse.bass as bass
import concourse.mybir as mybir
from concourse.bass_utils import run_bass_kernel_spmd

F32 = mybir.dt.float32
BF16 = mybir.dt.bfloat16
AF = mybir.ActivationFunctionType
ALU = mybir.AluOpType
AX = mybir.AxisListType

PE, ACT, DVE, POOL, SP = "pe", "act", "dve", "pool", "sp"
ENGS = [PE, ACT, DVE, POOL, SP]
EIDX = {e: i for i, e in enumerate(ENGS)}

D = 1024
SEQ = 4096
NCORES = 8
DFF = 2816
EPS = 1e-6


class Op:
    __slots__ = ("eng", "emit", "idx", "waits", "need_inc", "clock", "dma_sem", "dma_val", "tag")

    def __init__(self, eng, emit):
        self.eng = eng
        self.emit = emit
        self.waits = []
        self.need_inc = False
        self.dma_sem = None
        self.dma_val = 0


class MK:
    def __init__(self, nc):
        self.nc = nc
        self.ops = {e: [] for e in ENGS}
        self.last_w = {}
        self.readers = {}
        self.known = {e: [-1] * len(ENGS) for e in ENGS}
        self.known_dma = {e: {} for e in ENGS}
        self.dma_count = {}
        self.dma_keys = []
        self.pending = {e: [] for e in ENGS}
        self.stage = "setup"

    def _deps(self, reads, writes, eng=None):
        deps = []
        for k in reads:
            t = self.last_w.get(k)
            if t is not None:
                deps.append(t)
            if eng is not None and isinstance(k, tuple) and k[0] == "ps":
                for r in self.readers.get(k, ()):
                    if r[0] == "e" and r[1] != eng:
                        deps.append(r)
        for k in writes:
            t = self.last_w.get(k)
            if t is not None:
                deps.append(t)
            deps.extend(self.readers.get(k, ()))
        return deps

    def fence(self, engs, keys):
        deps = self._deps([], keys)
        for e in engs:
            self.pending[e].extend(deps)

    def _add(self, eng, emit, reads, writes, dma_sem=None):
        op = Op(eng, emit)
        op.tag = self.stage
        lst = self.ops[eng]
        op.idx = len(lst)
        deps = self._deps(reads, writes, eng)
        if self.pending[eng]:
            deps.extend(self.pending[eng])
            self.pending[eng] = []
        kn = self.known[eng]
        kd = self.known_dma[eng]
        best = {}
        for t in deps:
            k = (t[0], t[1])
            o = best.get(k)
            if o is None or t[2] > o[2]:
                best[k] = t
        for t in best.values():
            if t[0] == "e":
                _, de, di, dclk = t
                if de == eng and eng in (PE, SP):
                    continue
                j = EIDX[de]
                if kn[j] >= di:
                    continue
                if de == eng and op.idx - di >= 4:
                    continue
                op.waits.append(("e", de, di))
                self.ops[de][di].need_inc = True
                kn[j] = di
                for jj, v in enumerate(dclk):
                    if v > kn[jj]:
                        kn[jj] = v
            else:
                _, sk, val = t
                if kd.get(sk, 0) >= val:
                    continue
                assert val == self.dma_count[sk], ("ambiguous DMA semaphore wait", sk, val, self.dma_count[sk])
                op.waits.append(("d", sk, val))
                kd[sk] = val
        if dma_sem is not None:
            if dma_sem not in self.dma_count:
                self.dma_count[dma_sem] = 0
                self.dma_keys.append(dma_sem)
            self.dma_count[dma_sem] += 16
            op.dma_sem = dma_sem
            op.dma_val = self.dma_count[dma_sem]
            tok = ("d", dma_sem, op.dma_val)
        else:
            clk = list(kn)
            clk[EIDX[eng]] = op.idx
            tok = ("e", eng, op.idx, tuple(clk))
        for k in writes:
            self.last_w[k] = tok
            self.readers[k] = []
        for k in reads:
            if k not in writes:
                self.readers.setdefault(k, []).append(tok)
        lst.append(op)
        return op

    def op(self, eng, emit, reads=(), writes=()):
        return self._add(eng, emit, list(reads), list(writes))

    def dma(self, eng, sem, out, in_, reads=(), writes=()):
        def emit(e):
            return e.dma_start(out=out, in_=in_)
        return self._add(eng, emit, list(reads), list(writes), dma_sem=sem)

    def finish(self):
        nc = self.nc
        with ExitStack() as st:
            esem = {e: st.enter_context(nc.semaphore("s_" + e)) for e in ENGS}
            dsem = {k: st.enter_context(nc.semaphore("d_%d" % i)) for i, k in enumerate(self.dma_keys)}
            semval = {}
            for e in ENGS:
                c = 0
                vals = []
                for o in self.ops[e]:
                    if o.need_inc:
                        c += 1
                    vals.append(c)
                semval[e] = vals
            block = st.enter_context(nc.Block())

            def run(eng_name, e, final=False):
                for o in self.ops[eng_name]:
                    for w in o.waits:
                        if w[0] == "e":
                            e.wait_ge(esem[w[1]], semval[w[1]][w[2]])
                        else:
                            e.wait_ge(dsem[w[1]], w[2])
                    ins = o.emit(e)
                    if o.dma_sem is not None:
                        ins.then_inc(dsem[o.dma_sem], 16)
                    elif o.need_inc:
                        ins.then_inc(esem[eng_name], 1)
                if final:
                    for k in self.dma_keys:
                        e.wait_ge(dsem[k], self.dma_count[k])

            @block.tensor
            def _(e):
                run(PE, e)

            @block.scalar
            def _(e):
                run(ACT, e)

            @block.vector
            def _(e):
                run(DVE, e)

            @block.gpsimd
            def _(e):
                run(POOL, e)

            @block.sync
            def _(e):
                run(SP, e, final=True)


def qperm(h):
    hkv, g = divmod(h, 4)
    return (hkv // 2) * 4 + g, hkv % 2


STOP = [99]


def build_program(n_prompt_tiles=16, do_sample=True):
    nc = bass.Bass("TRN2", target_bir_lowering=False)
    dt_in = lambda name, shape: nc.dram_tensor(name, list(shape), F32, kind="ExternalInput").ap()
    dt_out = lambda name, shape: nc.dram_tensor(name, list(shape), F32, kind="ExternalOutput").ap()
    xp = dt_in("xp", [2, SEQ, D])
    xs = dt_in("xs", [32, D])
    ck = dt_in("ck", [2, 128, 256])
    cv = dt_in("cv", [2, 128, 256])
    g_in = [dt_in("g%d" % i, [1, D]) for i in range(4)]
    w_in = dt_in("w_in", [D, 5632])
    sinks = dt_in("sinks", [1, 16])
    ln_g = dt_in("ln_g", [1, D])
    ln_b = dt_in("ln_b", [1, D])
    w_s = dt_in("w_s", [4, 128, 128])
    b_s = dt_in("b_s", [1, 512])
    wap = dt_in("wap", [D, D])
    wgp = dt_in("wgp", [D, D])
    wout = dt_in("wout", [D, D])
    wfi = dt_in("wfi", [D, 5632])
    wfo = dt_in("wfo", [DFF, D])
    yp = dt_out("yp", [2, SEQ, D])
    ys = dt_out("ys", [32, D])
    kp = dt_out("kp", [2, 128, 256])
    vp = dt_out("vp", [2, 128, 256])
    ks = dt_out("ks", [2, 128, 256])
    vs = dt_out("vs", [2, 128, 256])
    gs = dt_out("gs", [32, D])
    sc = lambda name, shape: nc.dram_tensor(name, list(shape), BF16, kind="Internal").ap()
    s_win = sc("s_win", [128, 11, 8, 512])
    s_wap = sc("s_wap", [128, 2, 8, 512])
    s_wgp = sc("s_wgp", [128, 2, 8, 512])
    s_wout = sc("s_wout", [128, 2, 8, 512])
    s_wfi = sc("s_wfi", [128, 11, 8, 512])
    s_wfo = sc("s_wfo", [128, 2, 22, 512])

    with ExitStack() as st:
        sb = lambda name, shape, dt: st.enter_context(nc.sbuf_tensor(name, list(shape), dt))
        m = MK(nc)
        NRING = 8
        ring = sb("ring", [128, NRING, 8, 512], BF16)
        xt = sb("xt", [128, 6, D], F32)
        hT = sb("hT", [128, 8, 512], BF16)
        big = sb("big", [128, 24, 512], BF16)
        qT = big[:, 0:8, :]
        uT = big[:, 8:16, :]
        sgaT = big[:, 16:24, :]
        actT = big[:, 0:22, :]
        sgbT = sb("sgbT", [128, 8, 512], BF16)
        aT = sb("aT", [128, 8, 512], BF16)
        mgT = sb("mgT", [128, 8, 512], BF16)
        kT = sb("kT", [128, 2, 640], BF16)
        Vt = sb("Vt", [128, 5, 256], BF16)
        gvf = sb("gvf", [128, D], F32)
        gvn = sb("gvn", [128, 1, D], BF16)
        PT = sb("PT", [128, 8, 512], BF16)
        dent = sb("dent", [128, 2, 512], F32)
        junk = dent[:, 0, :].bitcast(BF16).rearrange("p (a b) -> p a b", a=2)
        hst = gvn
        tmpf = sb("tmpf", [128, 4, 512], F32)
        kvo = tmpf[:, 0, :]
        stt = sb("stt", [128, 64], F32)
        gS = sb("gS", [128, 4, D], F32)
        lng = sb("lng", [128, D], F32)
        lnb = sb("lnb", [128, D], F32)
        es = sb("es", [128, 16], F32)
        es2 = sb("es2", [128, 8], F32)
        negh = sb("negh", [128, 4], F32)
        ident = sb("ident", [128, 128], BF16)
        ones = sb("ones", [128, 128], BF16)
        WmT = sb("WmT", [128, 4, 128], BF16)
        WmT16 = sb("WmT16", [32, 4, 32], BF16)
        wsf = sb("wsf", [128, 128], F32)
        wsb = sb("wsb", [128, 128], BF16)
        bsb = sb("bsb", [1, 4, 128], BF16)
        bs16 = sb("bs16", [1, 4, 32], BF16)
        msk = sb("msk", [32, 2, 64], BF16)
        kTc = sb("kTc", [128, 2, 2, 128], BF16)
        Vc = sb("Vc", [128, 2, 256], BF16)
        ckf = sb("ckf", [128, 256], F32)
        ckb = sb("ckb", [128, 256], BF16)
        banks = [st.enter_context(nc.psum_tensor("ps%d" % i, [128, 512], F32)) for i in range(8)]
        bank_ctr = [0]

        kvst_ctr = [0]

        def kvsem():
            kvst_ctr[0] += 1
            return ("st_kv", kvst_ctr[0])

        def nb():
            i = 1 + bank_ctr[0] % 7
            bank_ctr[0] += 1
            return i

        def mm(b, out, lhsT, rhs, start, stop, reads):
            m.op(PE, lambda e: e.matmul(out, lhsT=lhsT, rhs=rhs, start=start, stop=stop), reads=reads,
                 writes=[("ps", b)])

        def act(out, in_, func, reads, writes, scale=1.0, accum_out=None):
            if accum_out is None:
                m.op(ACT, lambda e: e.activation(out=out, in_=in_, func=func, scale=scale), reads=reads, writes=writes)
            else:
                m.op(ACT, lambda e: e.activation(out=out, in_=in_, func=func, scale=scale, accum_out=accum_out),
                     reads=reads, writes=writes)

        def tcopy(eng, out, in_, reads, writes):
            if eng == ACT:
                m.op(eng, lambda e: e.activation(out=out, in_=in_, func=AF.Copy), reads=reads, writes=writes)
            else:
                m.op(eng, lambda e: e.tensor_copy(out=out, in_=in_), reads=reads, writes=writes)

        def tt(eng, out, in0, in1, op, reads, writes):
            m.op(eng, lambda e: e.tensor_tensor(out=out, in0=in0, in1=in1, op=op), reads=reads, writes=writes)

        def ts(eng, out, in0, s1, s2, op0, op1, reads, writes):
            if s2 is None:
                m.op(eng, lambda e: e.tensor_scalar(out=out, in0=in0, scalar1=s1, scalar2=None, op0=op0),
                     reads=reads, writes=writes)
            else:
                m.op(eng, lambda e: e.tensor_scalar(out=out, in0=in0, scalar1=s1, scalar2=s2, op0=op0, op1=op1),
                     reads=reads, writes=writes)

        def stt_op(eng, out, in0, scalar, in1, op0, op1, reads, writes):
            m.op(eng, lambda e: e.scalar_tensor_tensor(out=out, in0=in0, scalar=scalar, in1=in1, op0=op0, op1=op1),
                 reads=reads, writes=writes)

        def wsrc(w, c0, n):
            return w[:, c0:c0 + n].rearrange("(kc p) n -> p kc n", p=128)

        conv_jobs = {}
        blk_view = {}

        def cj(key, sem, idx, in_):
            conv_jobs.setdefault(key, []).append((sem, idx, in_))

        A_ = slice(None)
        for cb in range(11):
            blk_view[("s_win", cb)] = s_win[:, cb, :, :]
            blk_view[("s_wfi", cb)] = s_wfi[:, cb, :, :]
        for cb in range(2):
            blk_view[("s_wap", cb)] = s_wap[:, cb, :, :]
            blk_view[("s_wgp", cb)] = s_wgp[:, cb, :, :]
            blk_view[("s_wout", cb)] = s_wout[:, cb, :, :]
        for h in range(16):
            t, half = qperm(h)
            cb, j = divmod(t, 4)
            c0 = j * 128 + half * 64
            cj(("s_win", cb), ("cv_win", cb), (A_, A_, slice(c0, c0 + 64)), wsrc(w_in, h * 64, 64))
        for cb in range(2, 11):
            cj(("s_win", cb), ("cv_win", cb), (A_, A_, A_), wsrc(w_in, cb * 512, 512))
        for h in range(16):
            t, half = qperm(h)
            for cb in range(2):
                cj(("s_wap", cb), ("cv_wap", cb), (slice(half * 64, half * 64 + 64), t, A_),
                   wap[h * 64:(h + 1) * 64, cb * 512:(cb + 1) * 512])
        for cb in range(2):
            cj(("s_wgp", cb), ("cv_wgp", cb), (A_, A_, A_), wsrc(wgp, cb * 512, 512))
        for cb in range(2):
            cj(("s_wout", cb), ("cv_wout", cb), (A_, A_, A_), wsrc(wout, cb * 512, 512))
        for i in range(11):
            cj(("s_wfi", i), ("cv_wfi", i), (A_, A_, slice(0, 256)), wsrc(wfi, 256 * i, 256))
            cj(("s_wfi", i), ("cv_wfi", i), (A_, A_, slice(256, 512)), wsrc(wfi, DFF + 256 * i, 256))
        for n in range(2):
            for c, (k0, k1) in enumerate(((0, 8), (8, 16), (16, 22))):
                blk_view[("s_wfo", n, c)] = s_wfo[:, n, k0:k1, :]
                cj(("s_wfo", n, c), ("cv_wfo", n, c), (A_, A_, A_),
                   wfo[k0 * 128:k1 * 128, n * 512:(n + 1) * 512].rearrange("(kc p) n -> p kc n", p=128))
        conv_order = ([("s_win", c) for c in (0, 1, 2, 3, 4, 7, 5, 6, 8, 9, 10)]
                      + [("s_wap", 0), ("s_wgp", 0), ("s_wap", 1), ("s_wgp", 1), ("s_wout", 0), ("s_wout", 1)]
                      + [("s_wfi", i) for i in range(11)]
                      + [("s_wfo", n, c) for n in range(2) for c in range(3)])
        conv_ptr = [0]
        CONV_LOOKAHEAD = [14]

        def conv_upto(key):
            if conv_ptr[0] >= len(conv_order):
                return
            tgt = min(len(conv_order), conv_order.index(key) + 1 + CONV_LOOKAHEAD[0])
            while conv_ptr[0] < tgt:
                k = conv_order[conv_ptr[0]]
                for sem, idx, in_ in conv_jobs[k]:
                    m.dma(POOL, sem, blk_view[k][idx], in_, writes=[k])
                conv_ptr[0] += 1

        x_preloaded = [False]
        if n_prompt_tiles > 0:
            for b in range(4):
                m.dma(SP, ("xld", b), xt[:, b, :], xp[0, b * 128:(b + 1) * 128, :], writes=[("xt", b)])
            x_preloaded[0] = True

        for i in range(4):
            m.dma(SP, ("c_g", i), gS[:, i, :], g_in[i][0:1, :].partition_broadcast(128), writes=[("gS", i)])
            act(gS[:, i, :], gS[:, i, :], AF.Copy, [("gS", i)], [("gS", i)], scale=32.0)
        m.dma(SP, "c_lng", lng[:], ln_g[0:1, :].partition_broadcast(128), writes=["lng"])
        m.dma(SP, "c_lnb", lnb[:], ln_b[0:1, :].partition_broadcast(128), writes=["lnb"])
        m.dma(SP, "c_es", es[:], sinks[0:1, :].partition_broadcast(128), writes=["es"])
        act(es[:], es[:], AF.Exp, ["es"], ["es"])
        for tp_ in range(2):
            for hf in range(2):
                tcopy(DVE, es2[64 * hf:64 * hf + 64, 4 * tp_:4 * tp_ + 4],
                      es[64 * hf:64 * hf + 64, 4 * (2 * tp_ + hf):4 * (2 * tp_ + hf) + 4], ["es"], ["es2"])
        m.op(POOL, lambda e: e.memset(negh[:], -0.5), writes=["negh"])
        m.op(POOL, lambda e: e.memset(ones[:], 1.0), writes=["ones"])
        m.op(POOL, lambda e: e.memset(wsf[:], 1.0), writes=["wsf"])
        m.op(POOL, lambda e: e.affine_select(out=wsf[:], in_=wsf[:], pattern=[[-1, 128]],
                                             compare_op=ALU.is_equal, fill=0.0, base=0, channel_multiplier=1),
             reads=["wsf"], writes=["wsf"])
        tcopy(DVE, ident[:], wsf[:], ["wsf"], ["ident"])
        m.op(POOL, lambda e: e.memset(PT[:], 0.0), writes=[("PT", i) for i in range(8)])
        pT_b = 0
        pTv = banks[pT_b][:].bitcast(BF16)
        def setup_wmt():
            for g in range(4):
                m.dma(SP, "c_ws", wsf[:], w_s[g, :, :], writes=["wsf"])
                m.op(POOL, lambda e: e.affine_select(out=wsf[:], in_=wsf[:], pattern=[[-1, 128]], compare_op=ALU.is_ge,
                                                     fill=0.0, base=0, channel_multiplier=1),
                     reads=["wsf"], writes=["wsf"])
                tcopy(DVE, wsb[:], wsf[:], ["wsf"], ["wsb"])
                m.op(PE, lambda e: e.transpose(out=pTv[:, 0:128], in_=wsb[:], identity=ident[:]),
                     reads=["wsb", "ident"], writes=[("ps", pT_b)])
                tcopy(DVE, WmT[:, g, :], pTv[:, 0:128], [("ps", pT_b)], ["WmT"])
            m.dma(SP, "c_bs", kvo[0:1, :], b_s[0:1, :], writes=[("tmpf", 0)])
            tcopy(DVE, bsb[:].rearrange("p g t -> p (g t)"), kvo[0:1, :], [("tmpf", 0)], ["bsb"])

        def setup_sample_consts():
            if True:
                for g in range(4):
                    m.op(POOL, lambda e: e.memset(wsf[0:32, 0:32], 0.0), writes=["wsf"])
                    for s in range(2):
                        m.dma(SP, "c_ws", wsf[16 * s:16 * s + 16, 16 * s:16 * s + 16], w_s[g, 0:16, 0:16], reads=[],
                              writes=["wsf"])
                    m.op(POOL, lambda e: e.affine_select(out=wsf[0:32, 0:32], in_=wsf[0:32, 0:32], pattern=[[-1, 32]],
                                                         compare_op=ALU.is_ge, fill=0.0, base=0, channel_multiplier=1),
                         reads=["wsf"], writes=["wsf"])
                    tcopy(DVE, wsb[0:32, 0:32], wsf[0:32, 0:32], ["wsf"], ["wsb"])
                    m.op(PE, lambda e: e.transpose(out=pTv[0:32, 0:32], in_=wsb[0:32, 0:32], identity=ident[0:32, 0:32]),
                         reads=["wsb", "ident"], writes=[("ps", pT_b)])
                    tcopy(DVE, WmT16[:, g, :], pTv[0:32, 0:32], [("ps", pT_b)], ["WmT16"])
                for s in range(2):
                    tcopy(DVE, bs16[:, :, 16 * s:16 * s + 16], bsb[:, :, 0:16], ["bsb"], ["bs16"])
                m.op(POOL, lambda e: e.memset(msk[:], 1.0), writes=["msk"])
                m.op(POOL, lambda e: e.affine_select(out=msk[:, 0, :], in_=msk[:, 0, :], pattern=[[0, 64]],
                                                     compare_op=ALU.is_ge, fill=0.0, base=15, channel_multiplier=-1),
                     reads=["msk"], writes=["msk"])
                m.op(POOL, lambda e: e.affine_select(out=msk[:, 1, :], in_=msk[:, 1, :], pattern=[[0, 64]],
                                                     compare_op=ALU.is_ge, fill=0.0, base=-16, channel_multiplier=1),
                     reads=["msk"], writes=["msk"])


        ring_ctr = [0]

        FIRST_PASS = [False]

        def wload(src_ap, src_key, nkc=8):
            s = ring_ctr[0] % NRING
            ring_ctr[0] += 1
            if FIRST_PASS[0]:
                slot_v = ring[:, s, 0:nkc, :]
                for sem, idx, in_ in conv_jobs[src_key]:
                    m.dma(POOL, ("ring0", s), slot_v[idx], in_, writes=[("ring", s)])
                m.dma(SP, ("cvst", src_key), blk_view[src_key], slot_v, reads=[("ring", s)], writes=[src_key])
                return ring[:, s, :, :], ("ring", s)
            conv_upto(src_key)
            m.dma(SP, ("ring", s), ring[:, s, 0:nkc, :], src_ap, reads=[src_key], writes=[("ring", s)])
            return ring[:, s, :, :], ("ring", s)

        stat_ctr = [0]

        def stat(n=1):
            c = stat_ctr[0] % 64
            if c + n > 64:
                c = 0
            stat_ctr[0] = c + n
            return c

        tmp_ctr = [0]

        def ntmp():
            i = tmp_ctr[0] % 4
            tmp_ctr[0] += 1
            return i

        hst_ctr = [0]
        junk_ctr = [0]

        pre_done = [False]
        wmt_done = [False]

        PX = [POOL]
        YQ = [POOL]

        def rsqrt_op(P, out_ap, in_ap, rkeys, wkeys):
            if PX[0] == POOL:
                tt(POOL, out_ap, in_ap, negh[0:P, 0:1], ALU.pow, rkeys + ["negh"], wkeys)
            else:
                act(out_ap, in_ap, AF.Sqrt, rkeys, wkeys)
                m.op(DVE, lambda e: e.reciprocal(out=out_ap, in_=out_ap), reads=wkeys, writes=wkeys)

        def xkeys(xslot):
            return [("xt", xslot), ("xth", xslot, 0), ("xth", xslot, 1)]

        def norm_stats(P, xslot):
            c = stat(1)
            act(junk[0:P, :, :].rearrange("p a b -> p (a b)"), xt[0:P, xslot, :], AF.Square,
                xkeys(xslot), [("st", c), ("dent", 0)], accum_out=stt[0:P, c:c + 1])
            c2 = stat(2)
            ts(DVE, stt[0:P, c2:c2 + 1], stt[0:P, c:c + 1], float(D * EPS), None, ALU.add, None,
               [("st", c)], [("st", c2)])
            rsqrt_op(P, stt[0:P, c2 + 1:c2 + 2], stt[0:P, c2:c2 + 1], [("st", c2)], [("st", c2 + 1)])
            return c2 + 1

        HKEYS = [("gvn", 0, 0), ("gvn", 0, 1)]

        def norm_scale(P, gi, xslot, r):
            stt_op(DVE, hst[0:P, 0, :], xt[0:P, xslot, :], stt[0:P, r:r + 1], gS[0:P, gi, :], ALU.mult, ALU.mult,
                   xkeys(xslot) + [("st", r), ("gS", gi)], HKEYS)

        def norm_T(P, b):
            for kc in range(8):
                m.op(PE, (lambda kc: lambda e: e.transpose(out=pTv[:, kc * P:(kc + 1) * P],
                                                           in_=hst[0:P, 0, kc * 128:(kc + 1) * 128],
                                                           identity=ident[0:P, 0:P]))(kc),
                     reads=HKEYS + ["ident"], writes=[("ps", pT_b)])
            act(hT[:, :, b * P:(b + 1) * P], pTv[:, 0:8 * P].rearrange("p (k t) -> p k t", k=8), AF.Copy,
                [("ps", pT_b)], [("hT", b)])

        def norm_apply(P, b, gi, xslot, r):
            norm_scale(P, gi, xslot, r)
            norm_T(P, b)

        def norm_transpose(P, b, gi, xslot):
            norm_apply(P, b, gi, xslot, norm_stats(P, xslot))

        def xsrc(sample, seq, t0, b):
            return xs[:, :] if sample else xp[seq, t0 + b * 128:t0 + (b + 1) * 128, :]

        def run_pass(sample, seq=0, t0=0, xslots=(0,), nxt=None):
            P = 32 if sample else 128
            NB = 1 if sample else 4
            N = P * NB
            first_block_of_seq = (t0 == 0)
            last_tile_of_seq = (t0 + 512 == SEQ)
            hkeys = [("hT", b) for b in range(NB)]

            m.stage = "norm1"
            if not pre_done[0]:
                if x_preloaded[0] and not sample:
                    x_preloaded[0] = False
                else:
                    for b in range(NB):
                        m.dma(SP, ("xld", xslots[b]), xt[0:P, xslots[b], :], xsrc(sample, seq, t0, b),
                              writes=[("xt", xslots[b])])
                for b in range(NB):
                    norm_transpose(P, b, 0, xslots[b])
            pre_done[0] = False

            m.stage = "A.q"

            loop_banks = [None]
            lb_ctr = [0]
            S_BANKS = (1, 2, 6)
            sb_ctr = [0]

            def nbl():
                if loop_banks[0] is None:
                    return nb()
                i = loop_banks[0][lb_ctr[0] % 2]
                lb_ctr[0] += 1
                return i

            def ws_tile(wv, wkey, c0, dst, dkey, func, eng_copy=None, scale=1.0):
                b_ = nbl()
                for kc in range(8):
                    mm(b_, banks[b_][:, 0:N], wv[:, kc, c0:c0 + 128], hT[:, kc, 0:N], kc == 0, kc == 7,
                       [wkey] + hkeys)
                if eng_copy is not None:
                    tcopy(eng_copy, dst, banks[b_][:, 0:N], [("ps", b_)], [dkey])
                else:
                    act(dst, banks[b_][:, 0:N], func, [("ps", b_)], [dkey], scale=scale)

            for cb in range(2):
                wv, wk = wload(s_win[:, cb, :, :], ("s_win", cb))
                for j in range(4):
                    ws_tile(wv, wk, 128 * j, qT[:, 4 * cb + j, 0:N], ("qT", 4 * cb + j), AF.Copy)
            m.stage = "A.kv"
            wv, wk = wload(s_win[:, 2, :, :], ("s_win", 2))
            for j in range(2):
                if sample:
                    ws_tile(wv, wk, 128 * j, kT[:, j, 0:32], ("kT", 0), None, eng_copy=DVE)
                else:
                    ws_tile(wv, wk, 128 * j, kT[:, j, 128:640], ("kTw", j), None, eng_copy=DVE)
            if not sample:
                for sl in range(1, 5):
                    m.last_w[("kT", sl)] = m.last_w[("kTw", 1)]
                    m.readers[("kT", sl)] = []
            for b in range(NB):
                b_ = nb()
                need_k = (not sample) and last_tile_of_seq and b == NB - 1
                c0 = 0 if need_k else 256
                for kc in range(8):
                    mm(b_, banks[b_][0:P, c0:512], hT[:, kc, b * P:(b + 1) * P], wv[:, kc, c0:512], kc == 0, kc == 7,
                       [wk, ("hT", b)])
                slot = 0 if sample else b + 1
                tcopy(DVE, Vt[0:P, slot, :], banks[b_][0:P, 256:512], [("ps", b_)], [("Vt", slot)])
                if sample:
                    for s in range(2):
                        bq = nb()
                        for kc in range(8):
                            mm(bq, banks[bq][0:16, :], hT[:, kc, 16 * s:16 * s + 16], wv[:, kc, :], kc == 0, kc == 7,
                               [wk, ("hT", b)])
                        tcopy(ACT, kvo[0:16, :], banks[bq][0:16, :], [("ps", bq)], [("tmpf", 0)])
                        m.dma(POOL, kvsem(), ks[s, 112:128, :], kvo[0:16, 0:256], reads=[("tmpf", 0)])
                        m.dma(POOL, kvsem(), vs[s, 112:128, :], kvo[0:16, 256:512], reads=[("tmpf", 0)])
                        for src_c, dst_c in ((ck, ks), (cv, vs)):
                            m.dma(SP, "c_ck", ckf[0:112, :], src_c[s, 16:128, :], writes=["ckf"])
                            m.dma(POOL, kvsem(), dst_c[s, 0:112, :], ckf[0:112, :], reads=["ckf"])
                elif last_tile_of_seq and b == NB - 1:
                    tcopy(ACT, kvo[:, :], banks[b_][:, :], [("ps", b_)], [("tmpf", 0)])
                    m.dma(POOL, kvsem(), kp[seq, :, :], kvo[:, 0:256], reads=[("tmpf", 0)])
                    m.dma(POOL, kvsem(), vp[seq, :, :], kvo[:, 256:512], reads=[("tmpf", 0)])
            m.stage = "A.u"
            for cb in (3, 4):
                wv, wk = wload(s_win[:, cb, :, :], ("s_win", cb))
                for j in range(4):
                    f = 4 * (cb - 3) + j
                    ws_tile(wv, wk, 128 * j, uT[:, f, 0:N], ("uT", f), AF.Gelu_apprx_tanh)
            if not wmt_done[0]:
                m.stage = "setup"
                setup_wmt()
                wmt_done[0] = True
            ukeys = [("uT", j) for j in range(8)]
            qkeys = [("qT", j) for j in range(8)]
            akeys = [("aT", j) for j in range(8)]

            gvw = {}

            def gv_mm(b):
                m.stage = "A.gv"
                if not gvw:
                    gvw[5] = wload(s_win[:, 5, :, :], ("s_win", 5))
                    gvw[6] = wload(s_win[:, 6, :, :], ("s_win", 6))
                for hh in range(2):
                    wv, wk = gvw[5 + hh]
                    b_ = nbl()
                    for kc in range(8):
                        mm(b_, banks[b_][0:P, :], hT[:, kc, b * P:(b + 1) * P], wv[:, kc, :], kc == 0, kc == 7,
                           [wk, ("hT", b)])
                    act(gvf[0:P, hh * 512:(hh + 1) * 512], banks[b_][0:P, :], AF.Gelu_apprx_tanh, [("ps", b_)],
                        [("gvf", hh)])
                c = stat(16)
                for hh in range(2):
                    m.op(DVE, (lambda hh, c: lambda e: e.bn_stats(out=stt[0:P, c + 6 * hh:c + 6 * hh + 6],
                                                                  in_=gvf[0:P, hh * 512:(hh + 1) * 512]))(hh, c),
                         reads=[("gvf", hh)], writes=[("st", c + 6 * hh)])
                m.op(DVE, (lambda c: lambda e: e.bn_aggr(out=stt[0:P, c + 12:c + 14], in_=stt[0:P, c:c + 12]))(c),
                     reads=[("st", c), ("st", c + 6)], writes=[("st", c + 12)])
                ts(DVE, stt[0:P, c + 14:c + 15], stt[0:P, c + 13:c + 14], float(EPS), None, ALU.add, None,
                   [("st", c + 12)], [("st", c + 14)])
                rsqrt_op(P, stt[0:P, c + 15:c + 16], stt[0:P, c + 14:c + 15], [("st", c + 14)], [("st", c + 15)])
                gk = [("gvf", 0), ("gvf", 1)]
                gslot = 0
                for hh, eng in ((0, DVE), (1, DVE)):
                    cs = slice(hh * 512, (hh + 1) * 512)
                    gkh = [("gvf", hh)]
                    ts(DVE, gvf[0:P, cs], gvf[0:P, cs], stt[0:P, c + 12:c + 13], stt[0:P, c + 15:c + 16],
                       ALU.subtract, ALU.mult, gkh + [("st", c + 12), ("st", c + 15)], gkh)
                    tt(eng, gvf[0:P, cs], gvf[0:P, cs], lng[0:P, cs], ALU.mult, gkh + ["lng"], gkh)
                    if sample:
                        tt(eng, gvf[0:P, cs], gvf[0:P, cs], lnb[0:P, cs], ALU.add, gkh + ["lnb"], gkh)
                    else:
                        tt(eng, gvn[0:P, gslot, cs], gvf[0:P, cs], lnb[0:P, cs], ALU.add, gkh + ["lnb"],
                           [("gvn", gslot, hh)])
                if sample:
                    m.dma(POOL, "st_gs", gs[:, :], gvf[0:P, :], reads=gk)
                    for hh in range(2):
                        cs = slice(hh * 512, (hh + 1) * 512)
                        tcopy(DVE, gvn[0:P, gslot, cs], gvf[0:P, cs], [("gvf", hh)], [("gvn", gslot, hh)])

            def spatial(b):
                m.stage = "A.sp"
                gslot = 0
                wmt = WmT16 if sample else WmT
                bsx = bs16 if sample else bsb
                for half in range(1 if sample else 2):
                    b_ = nbl()
                    jr = range(8) if sample else range(4 * half, 4 * half + 4)
                    for jj, j in enumerate(jr):
                        g = j // 2
                        o = banks[b_][:, jj * P:(jj + 1) * P]
                        mm(b_, o, gvn[0:P, gslot, j * 128:(j + 1) * 128], wmt[0:P, g, :], True, False,
                           [("gvn", gslot, j // 4), "WmT", "WmT16"])
                        mm(b_, o, ones[0:1, 0:128], bsx[0:1, g, :], False, True, ["ones", "bsb", "bs16"])
                    nj = len(jr)
                    j0 = jr[0]
                    tt(DVE, uT[:, j0:j0 + nj, b * P:(b + 1) * P],
                       banks[b_][:, 0:nj * P].rearrange("p (j t) -> p j t", j=nj),
                       uT[:, j0:j0 + nj, b * P:(b + 1) * P], ALU.mult,
                       [("ps", b_)] + ukeys[j0:j0 + nj], ukeys[j0:j0 + nj])

            gw = {}

            def gate_tile(n):
                m.stage = "A.gates"
                cb = 7 + n // 4
                if cb not in gw:
                    gw.clear()
                    gw[cb] = wload(s_win[:, cb, :, :], ("s_win", cb))
                wv, wk = gw[cb]
                if n < 8:
                    ws_tile(wv, wk, 128 * (n % 4), sgaT[:, n, 0:N], ("sga", n), AF.Tanh, scale=0.5)
                else:
                    ws_tile(wv, wk, 128 * (n % 4), sgbT[:, n - 8, 0:N], ("sgb", n - 8), AF.Tanh, scale=0.5)

            if sample:
                m.stage = "attn"
                for s in range(2):
                    m.dma(SP, "c_ck", ckf[:], ck[s, :, :], writes=["ckf"])
                    tcopy(DVE, ckb[:], ckf[:], ["ckf"], ["ckb"])
                    for j in range(2):
                        m.op(PE, (lambda j: lambda e: e.transpose(out=pTv[:, j * 128:(j + 1) * 128],
                                                                  in_=ckb[:, j * 128:(j + 1) * 128],
                                                                  identity=ident[:]))(j),
                             reads=["ckb", "ident"], writes=[("ps", pT_b)])
                    tcopy(DVE, kTc[:, s, :, :], pTv[:, 0:256].rearrange("p (j t) -> p j t", j=2), [("ps", pT_b)],
                          [("kTc", s)])
                    m.dma(SP, "c_ck", ckf[:], cv[s, :, :], writes=["ckf"])
                    tcopy(DVE, Vc[:, s, :], ckf[:], ["ckf"], [("Vc", s)])

            def attn_S(n, b, tp, half):
                m.stage = "attn"
                hs0 = half * 64
                pp = (n % 2) * 4 + half * 2
                first = (not sample) and first_block_of_seq and b == 0
                b1 = None
                if not first:
                    b1 = S_BANKS[sb_ctr[0] % 3]
                    sb_ctr[0] += 1
                b2 = S_BANKS[sb_ctr[0] % 3]
                sb_ctr[0] += 1
                if sample:
                    s_ = b
                    qv = qT[hs0:hs0 + 64, 4 * tp:4 * tp + 4, 16 * s_:16 * s_ + 16]
                    o1 = banks[b1][:, 0:64].rearrange("p (g q) -> p g q", g=4)
                    o2 = banks[b2][0:32, 0:64].rearrange("p (g q) -> p g q", g=4)
                    mm(b1, o1, kTc[hs0:hs0 + 64, s_, tp, :], qv, True, True, [("kTc", s_)] + qkeys)
                    mm(b2, o2, kT[hs0:hs0 + 64, tp, 0:32], qv, True, True, [("kT", 0)] + qkeys)
                    P1 = PT[:, pp, 0:64]
                    P2 = PT[0:32, pp + 1, 0:64]
                    act(P1, banks[b1][:, 0:64], AF.Exp, [("ps", b1)], [("PT", pp)], scale=0.125)
                    act(P2, banks[b2][0:32, 0:64], AF.Exp, [("ps", b2)], [("PT", pp + 1)], scale=0.125)
                    tt(DVE, P2, P2, msk[:, s_, :], ALU.mult, [("PT", pp + 1), "msk"], [("PT", pp + 1)])
                    return
                qv = qT[hs0:hs0 + 64, 4 * tp:4 * tp + 4, b * 128:(b + 1) * 128]
                P1 = PT[:, pp, :].rearrange("p (g q) -> p g q", g=4)
                P2 = PT[:, pp + 1, :].rearrange("p (g q) -> p g q", g=4)
                o1 = banks[b1][:, :].rearrange("p (g q) -> p g q", g=4) if b1 is not None else None
                o2 = banks[b2][:, :].rearrange("p (g q) -> p g q", g=4)
                if not first:
                    mm(b1, o1, kT[hs0:hs0 + 64, tp, b * 128:(b + 1) * 128], qv, True, True, [("kT", b)] + qkeys)
                mm(b2, o2, kT[hs0:hs0 + 64, tp, (b + 1) * 128:(b + 2) * 128], qv, True, True, [("kT", b + 1)] + qkeys)
                if not first:
                    act(P1[0:64, :, 0:64], o1[0:64, :, 0:64], AF.Exp, [("ps", b1)], [("PT", pp)], scale=0.125)
                    act(P1[64:128, :, :], o1[64:128, :, :], AF.Exp, [("ps", b1)], [("PT", pp)], scale=0.125)
                act(P2[0:64, :, :], o2[0:64, :, :], AF.Exp, [("ps", b2)], [("PT", pp + 1)], scale=0.125)
                act(P2[64:128, :, 64:128], o2[64:128, :, 64:128], AF.Exp, [("ps", b2)], [("PT", pp + 1)], scale=0.125)

            def attn_PV(n, b, tp):
                m.stage = "attn"
                d_i = n % 2
                bo, bd = (3 if n % 2 == 0 else 4), 5
                W = 64 if sample else 512
                NQ = 16 if sample else 128
                if sample:
                    for half in range(2):
                        hs0 = half * 64
                        pp = (n % 2) * 4 + half * 2
                        vc0 = 128 * tp + 64 * half
                        oo = banks[bo][hs0:hs0 + 64, 0:W]
                        od = banks[bd][hs0:hs0 + 64, 0:W]
                        s_ = b
                        P1 = PT[:, pp, 0:64]
                        P2 = PT[0:32, pp + 1, 0:64]
                        mm(bo, oo, Vc[:, s_, vc0:vc0 + 64], P1, True, False, [("Vc", s_), ("PT", pp)])
                        mm(bo, oo, Vt[0:32, 0, vc0:vc0 + 64], P2, False, True, [("Vt", 0), ("PT", pp + 1)])
                        mm(bd, od, ones[:, 0:64], P1, True, False, ["ones", ("PT", pp)])
                        mm(bd, od, ones[0:32, 0:64], P2, False, True, ["ones", ("PT", pp + 1)])
                else:
                    first = first_block_of_seq and b == 0
                    for bank_i, use_v in ((bo, True), (bd, False)):
                        for which in ((1, 2) if not first else (2,)):
                            for half in range(2):
                                hs0 = half * 64
                                pp = (n % 2) * 4 + half * 2 + (which - 1)
                                vc0 = 128 * tp + 64 * half
                                slot = b if which == 1 else b + 1
                                lhs = Vt[:, slot, vc0:vc0 + 64] if use_v else ones[:, 0:64]
                                rk = [("Vt", slot), ("PT", pp)] if use_v else ["ones", ("PT", pp)]
                                mm(bank_i, banks[bank_i][hs0:hs0 + 64, :], lhs, PT[:, pp, :],
                                   which == 1 or first, which == 2, rk)
                if sample:
                    a_out = aT[:, 4 * tp:4 * tp + 4, 16 * b:16 * b + 16]
                else:
                    a_out = aT[:, 4 * tp:4 * tp + 4, b * 128:(b + 1) * 128]
                dv = dent[:, d_i, 0:W]
                esb = es2[:, 4 * tp:4 * tp + 4].unsqueeze(2).to_broadcast([128, 4, NQ])
                tt(DVE, dv.rearrange("p (g q) -> p g q", g=4),
                   banks[bd][:, 0:W].rearrange("p (g q) -> p g q", g=4), esb, ALU.add,
                   [("ps", bd), "es2"], [("dent", d_i)])
                m.op(DVE, (lambda dv: lambda e: e.reciprocal(out=dv, in_=dv))(dv), reads=[("dent", d_i)],
                     writes=[("dent", d_i)])
                tt(DVE, a_out, banks[bo][:, 0:W].rearrange("p (g q) -> p g q", g=4),
                   dv.rearrange("p (g q) -> p g q", g=4), ALU.mult, [("ps", bo), ("dent", d_i)],
                   akeys[4 * tp:4 * tp + 4])

            pairs = [(b, tp) for b in range(2 if sample else NB) for tp in range(2)]
            NG = len(pairs)
            gpg = 16 // NG
            loop_banks[0] = (7, 0)
            attn_S(0, *pairs[0], 0)
            attn_S(0, *pairs[0], 1)
            for n, (b, tp) in enumerate(pairs):
                if n + 1 < NG:
                    attn_S(n + 1, *pairs[n + 1], 0)
                attn_PV(n, b, tp)
                if n + 1 < NG:
                    attn_S(n + 1, *pairs[n + 1], 1)
                for t in range(gpg):
                    gate_tile(gpg * n + t)
                if sample:
                    if n == 2:
                        spatial(0)
                    if n == 0:
                        gv_mm(0)
                else:
                    if n % 2 == 0 and n >= 2:
                        spatial(n // 2 - 1)
                    if n % 2 == 0:
                        gv_mm(n // 2)
            if not sample:
                spatial(NB - 1)
            loop_banks[0] = None
            if not sample:
                m.stage = "attn"
                tcopy(PX[0], kT[:, :, 0:128], kT[:, :, 512:640], [("kT", 4)], [("kT", 0)])
                tcopy(PX[0], Vt[:, 0, :], Vt[:, 4, :], [("Vt", 4)], [("Vt", 0)])
                m.fence([DVE], [("kT", sl) for sl in range(1, 5)])

            m.stage = "merge"
            for cbm in range(2):
                wva, wka = wload(s_wap[:, cbm, :, :], ("s_wap", cbm))
                wvg, wkg = wload(s_wgp[:, cbm, :, :], ("s_wgp", cbm))
                for j in range(4):
                    f = 4 * cbm + j
                    ba, bb = nb(), nb()
                    for kc in range(8):
                        mm(ba, banks[ba][:, 0:N], wva[:, kc, j * 128:(j + 1) * 128], aT[:, kc, 0:N], kc == 0, kc == 7,
                           [wka] + akeys)
                    for kc in range(8):
                        mm(bb, banks[bb][:, 0:N], wvg[:, kc, j * 128:(j + 1) * 128], uT[:, kc, 0:N], kc == 0, kc == 7,
                           [wkg] + ukeys)
                    t1, t2 = ntmp(), ntmp()
                    stt_op(DVE, tmpf[:, t1, 0:N], sgaT[:, f, 0:N], 1.0, banks[ba][:, 0:N], ALU.add, ALU.mult,
                           [("ps", ba), ("sga", f)], [("tmpf", t1)])
                    stt_op(DVE, tmpf[:, t2, 0:N], sgbT[:, f, 0:N], 1.0, banks[bb][:, 0:N], ALU.add, ALU.mult,
                           [("ps", bb), ("sgb", f)], [("tmpf", t2)])
                    tt(DVE, mgT[:, f, 0:N], tmpf[:, t1, 0:N], tmpf[:, t2, 0:N], ALU.add,
                       [("tmpf", t1), ("tmpf", t2)], [("mgT", f)])
            mkeys = [("mgT", f) for f in range(8)]

            def post_norm_residual(b, bks, gi, xslot, eps_mul=1.0, store_dst=None):
                c = stat(2)
                for n in range(2):
                    jk = junk_ctr[0] % 2
                    junk_ctr[0] += 1
                    act(junk[0:P, jk, :], banks[bks[n]][0:P, :], AF.Square, [("ps", bks[n])],
                        [("st", c + n), ("dent", 0)], accum_out=stt[0:P, c + n:c + n + 1])
                c2 = stat(2)
                ts(DVE, stt[0:P, c2:c2 + 1], stt[0:P, c:c + 1], stt[0:P, c + 1:c + 2], float(D * EPS * eps_mul),
                   ALU.add, ALU.add, [("st", c), ("st", c + 1)], [("st", c2)])
                rsqrt_op(P, stt[0:P, c2 + 1:c2 + 2], stt[0:P, c2:c2 + 1], [("st", c2)], [("st", c2 + 1)])
                for n in range(2):
                    t1 = ntmp()
                    stt_op(DVE, tmpf[0:P, t1, :], banks[bks[n]][0:P, :], stt[0:P, c2 + 1:c2 + 2],
                           gS[0:P, gi, n * 512:(n + 1) * 512], ALU.mult, ALU.mult,
                           [("ps", bks[n]), ("st", c2 + 1), ("gS", gi)], [("tmpf", t1)])
                    if store_dst is None:
                        tt(PX[0] if n == 0 else DVE, xt[0:P, xslot, n * 512:(n + 1) * 512],
                           xt[0:P, xslot, n * 512:(n + 1) * 512], tmpf[0:P, t1, :], ALU.add,
                           [("tmpf", t1), ("xt", xslot), ("xth", xslot, n)], [("xth", xslot, n)])
                    else:
                        tt(PX[0] if n == 0 else DVE, tmpf[0:P, t1, :], xt[0:P, xslot, n * 512:(n + 1) * 512],
                           tmpf[0:P, t1, :], ALU.add, [("tmpf", t1), ("xt", xslot), ("xth", xslot, n)],
                           [("tmpf", t1)])
                        m.dma(YQ[0], ("yst", t1, YQ[0]), store_dst[:, n * 512:(n + 1) * 512], tmpf[0:P, t1, :],
                              reads=[("tmpf", t1)])

            m.stage = "wout"
            wvo = [wload(s_wout[:, n, :, :], ("s_wout", n)) for n in range(2)]
            wbks = []
            r3 = {}

            def pn(b):
                m.stage = "wout"
                post_norm_residual(b, wbks[b], 1, xslots[b], eps_mul=4.0)

            def s3(b):
                m.stage = "norm3"
                r3[b] = norm_stats(P, xslots[b])

            def a3(b):
                m.stage = "norm3"
                norm_apply(P, b, 2, xslots[b], r3[b])

            def wout_mm(b):
                m.stage = "wout"
                bks = [nb(), nb()]
                wbks.append(bks)
                for n in range(2):
                    for kc in range(8):
                        mm(bks[n], banks[bks[n]][0:P, :], mgT[:, kc, b * P:(b + 1) * P], wvo[n][0][:, kc, :], kc == 0,
                           kc == 7, [wvo[n][1]] + mkeys)

            if NB == 1:
                wout_mm(0); pn(0); s3(0); a3(0)
            else:
                def a3s(b):
                    m.stage = "norm3"
                    norm_scale(P, 2, xslots[b], r3[b])

                def a3t(b):
                    m.stage = "norm3"
                    norm_T(P, b)

                wout_mm(0); wout_mm(1); pn(0)
                wout_mm(2); pn(1); s3(0); pn(2)
                wout_mm(3); a3s(0); s3(1); pn(3)
                a3t(0); a3s(1); s3(2)
                a3t(1); a3s(2); s3(3)
                a3t(2); a3s(3)
                a3t(3)

            m.stage = "ffn_in"
            if nxt is not None:
                for bb_ in range(2):
                    m.dma(SP, ("xld", nxt["xslots"][bb_]), xt[:, nxt["xslots"][bb_], :],
                          xsrc(False, nxt["seq"], nxt["t0"], bb_), writes=[("xt", nxt["xslots"][bb_])])
            m.fence([DVE], qkeys + ukeys + [("sga", j) for j in range(8)])
            for i in range(11):
                wv, wk = wload(s_wfi[:, i, :, :], ("s_wfi", i))
                for jj in range(2):
                    bg, bu = nb(), nb()
                    for kc in range(8):
                        mm(bg, banks[bg][:, 0:N], wv[:, kc, jj * 128:(jj + 1) * 128], hT[:, kc, 0:N], kc == 0, kc == 7,
                           [wk] + hkeys)
                    for kc in range(8):
                        mm(bu, banks[bu][:, 0:N], wv[:, kc, 256 + jj * 128:256 + (jj + 1) * 128], hT[:, kc, 0:N],
                           kc == 0, kc == 7, [wk] + hkeys)
                    t1 = ntmp()
                    act(tmpf[:, t1, 0:N], banks[bg][:, 0:N], AF.Silu, [("ps", bg)], [("tmpf", t1)])
                    tt(DVE, actT[:, 2 * i + jj, 0:N], banks[bu][:, 0:N], tmpf[:, t1, 0:N], ALU.mult,
                       [("ps", bu), ("tmpf", t1)], [("actT", 2 * i + jj)])
            fkeys = [("actT", j) for j in range(22)]

            m.stage = "ffn_out"
            chunks = ((0, 8), (8, 16), (16, 22))
            wfo_v = [[wload(s_wfo[:, n, k0:k1, :], ("s_wfo", n, c), nkc=k1 - k0) for c, (k0, k1) in enumerate(chunks)]
                     for n in range(2)]
            rn = {}
            for b in range(NB):
                m.stage = "ffn_out"
                if nxt is not None and 1 <= b <= 2:
                    m.dma(SP, ("xld", nxt["xslots"][b + 1]), xt[:, nxt["xslots"][b + 1], :],
                          xsrc(False, nxt["seq"], nxt["t0"], b + 1), writes=[("xt", nxt["xslots"][b + 1])])
                bks = [nb(), nb()]
                for n in range(2):
                    for c, (k0, k1) in enumerate(chunks):
                        wv, wk = wfo_v[n][c]
                        for kc in range(k0, k1):
                            mm(bks[n], banks[bks[n]][0:P, :], actT[:, kc, b * P:(b + 1) * P], wv[:, kc - k0, :],
                               kc == 0, kc == 21, [wk] + fkeys)
                if nxt is not None:
                    m.stage = "norm1"
                    if b == 0:
                        rn[0] = norm_stats(128, nxt["xslots"][0])
                        rn[1] = norm_stats(128, nxt["xslots"][1])
                    norm_scale(128, 0, nxt["xslots"][b], rn[b])
                    norm_T(128, b)
                    if 1 <= b <= 2:
                        rn[b + 1] = norm_stats(128, nxt["xslots"][b + 1])
                    m.stage = "ffn_out"
                dst = ys[:, :] if sample else yp[seq, t0 + b * 128:t0 + (b + 1) * 128, :]
                post_norm_residual(b, bks, 3, xslots[b], store_dst=dst)
            m.fence([ACT, DVE], fkeys)
            if nxt is not None:
                pre_done[0] = True

        tiles = [dict(seq=i // 8, t0=(i % 8) * 512, xslots=[(4 * i + b) % 6 for b in range(4)])
                 for i in range(n_prompt_tiles)]
        for i, t in enumerate(tiles):
            nxt = tiles[i + 1] if i + 1 < len(tiles) else None
            if i == 0:
                PX[0], YQ[0], FIRST_PASS[0] = DVE, ACT, True
            run_pass(False, seq=t["seq"], t0=t["t0"], xslots=t["xslots"], nxt=nxt)
            if i == 0:
                conv_ptr[0] = len(conv_order)
            PX[0], YQ[0], FIRST_PASS[0] = POOL, POOL, False
        if do_sample:
            m.stage = "setup"
            setup_sample_consts()
            run_pass(True, xslots=(0,))
        m.finish()
        nc._mk_tags = {e: [o.tag for o in m.ops[e]] for e in ENGS}
    return nc


_NC_CACHE = {}


def _get_nc():
    if "nc" not in _NC_CACHE:
        _NC_CACHE["nc"] = build_program()
    return _NC_CACHE["nc"]


def make_in_maps(inputs):
    f = lambda a: np.ascontiguousarray(np.asarray(a, dtype=np.float32))
    x_prompt = f(inputs["x_prompt"])
    x_sample = f(inputs["x_sample"])
    ck = f(inputs["cache_attn_k"])[0].reshape(16, 128, 256)
    cv = f(inputs["cache_attn_v"])[0].reshape(16, 128, 256)
    shared = {
        "g0": f(inputs["norm_pre_mix"]).reshape(1, D),
        "g1": f(inputs["norm_post_mix"]).reshape(1, D),
        "g2": f(inputs["norm_pre_ffn"]).reshape(1, D),
        "g3": f(inputs["norm_post_ffn"]).reshape(1, D),
        "w_in": f(inputs["w_in"])[0],
        "sinks": f(inputs["attn_sinks"]).reshape(1, 16),
        "ln_g": f(inputs["gmlp_ln_g"]).reshape(1, D),
        "ln_b": f(inputs["gmlp_ln_b"]).reshape(1, D),
        "w_s": f(inputs["gmlp_w_s"])[0],
        "b_s": f(inputs["gmlp_b_s"]).reshape(1, 512),
        "wap": f(inputs["w_attn_proj"])[0],
        "wgp": f(inputs["w_gmlp_proj"])[0],
        "wout": f(inputs["w_out"])[0],
        "wfi": f(inputs["w_ffn_in"])[0],
        "wfo": f(inputs["w_ffn_out"])[0],
    }
    in_maps = []
    for c in range(NCORES):
        d = dict(shared)
        d["xp"] = x_prompt[2 * c:2 * c + 2]
        d["xs"] = x_sample[2 * c:2 * c + 2].reshape(32, D)
        d["ck"] = ck[2 * c:2 * c + 2]
        d["cv"] = cv[2 * c:2 * c + 2]
        in_maps.append(d)
    return in_maps


def gather(results):
    cat = lambda k: np.concatenate([np.asarray(r[k]) for r in results], axis=0)
    y_prompt = cat("yp").reshape(16, SEQ, D).astype(np.float32)
    y_sample = cat("ys").reshape(16, 16, D).astype(np.float32)
    kp = cat("kp").reshape(1, 16, 128, 4, 64).astype(np.float32)
    vp = cat("vp").reshape(1, 16, 128, 4, 64).astype(np.float32)
    ks = cat("ks").reshape(1, 16, 128, 4, 64).astype(np.float32)
    vs = cat("vs").reshape(1, 16, 128, 4, 64).astype(np.float32)
    gs = cat("gs").reshape(1, 16, 16, D).astype(np.float32)
    return (y_prompt, y_sample, kp, vp, ks, vs, gs)


def kernel(**inputs):
    nc = _get_nc()
    in_maps = make_in_maps(inputs)
    res = run_bass_kernel_spmd(nc, in_maps, core_ids=list(range(NCORES)))
    return gather(res.results)
```

```python
import numpy as np
from contextlib import ExitStack
import concourse.bass as bass
import concourse.mybir as mybir
from concourse.bass_utils import run_bass_kernel_spmd

F32 = mybir.dt.float32
BF16 = mybir.dt.bfloat16
AF = mybir.ActivationFunctionType
ALU = mybir.AluOpType
AX = mybir.AxisListType

PE, ACT, DVE, POOL, SP = "pe", "act", "dve", "pool", "sp"
ENGS = [PE, ACT, DVE, POOL, SP]
EIDX = {e: i for i, e in enumerate(ENGS)}

D = 1024
SEQ = 4096
NCORES = 8
DFF = 2816
EPS = 1e-6


class Op:
    __slots__ = ("eng", "emit", "idx", "waits", "need_inc", "clock", "dma_sem", "dma_val", "tag")

    def __init__(self, eng, emit):
        self.eng = eng
        self.emit = emit
        self.waits = []
        self.need_inc = False
        self.dma_sem = None
        self.dma_val = 0


class MK:
    def __init__(self, nc):
        self.nc = nc
        self.ops = {e: [] for e in ENGS}
        self.last_w = {}
        self.readers = {}
        self.known = {e: [-1] * len(ENGS) for e in ENGS}
        self.known_dma = {e: {} for e in ENGS}
        self.dma_count = {}
        self.dma_keys = []
        self.pending = {e: [] for e in ENGS}
        self.stage = "setup"

    def _deps(self, reads, writes, eng=None):
        deps = []
        for k in reads:
            t = self.last_w.get(k)
            if t is not None:
                deps.append(t)
            if eng is not None and isinstance(k, tuple) and k[0] == "ps":
                for r in self.readers.get(k, ()):
                    if r[0] == "e" and r[1] != eng:
                        deps.append(r)
        for k in writes:
            t = self.last_w.get(k)
            if t is not None:
                deps.append(t)
            deps.extend(self.readers.get(k, ()))
        return deps

    def fence(self, engs, keys):
        deps = self._deps([], keys)
        for e in engs:
            self.pending[e].extend(deps)

    def _add(self, eng, emit, reads, writes, dma_sem=None):
        op = Op(eng, emit)
        op.tag = self.stage
        lst = self.ops[eng]
        op.idx = len(lst)
        deps = self._deps(reads, writes, eng)
        if self.pending[eng]:
            deps.extend(self.pending[eng])
            self.pending[eng] = []
        kn = self.known[eng]
        kd = self.known_dma[eng]
        best = {}
        for t in deps:
            k = (t[0], t[1])
            o = best.get(k)
            if o is None or t[2] > o[2]:
                best[k] = t
        for t in best.values():
            if t[0] == "e":
                _, de, di, dclk = t
                if de == eng and eng in (PE, SP):
                    continue
                j = EIDX[de]
                if kn[j] >= di:
                    continue
                if de == eng and op.idx - di >= 4:
                    continue
                op.waits.append(("e", de, di))
                self.ops[de][di].need_inc = True
                kn[j] = di
                for jj, v in enumerate(dclk):
                    if v > kn[jj]:
                        kn[jj] = v
            else:
                _, sk, val = t
                if kd.get(sk, 0) >= val:
                    continue
                assert val == self.dma_count[sk], ("ambiguous DMA semaphore wait", sk, val, self.dma_count[sk])
                op.waits.append(("d", sk, val))
                kd[sk] = val
        if dma_sem is not None:
            if dma_sem not in self.dma_count:
                self.dma_count[dma_sem] = 0
                self.dma_keys.append(dma_sem)
            self.dma_count[dma_sem] += 16
            op.dma_sem = dma_sem
            op.dma_val = self.dma_count[dma_sem]
            tok = ("d", dma_sem, op.dma_val)
        else:
            clk = list(kn)
            clk[EIDX[eng]] = op.idx
            tok = ("e", eng, op.idx, tuple(clk))
        for k in writes:
            self.last_w[k] = tok
            self.readers[k] = []
        for k in reads:
            if k not in writes:
                self.readers.setdefault(k, []).append(tok)
        lst.append(op)
        return op

    def op(self, eng, emit, reads=(), writes=()):
        return self._add(eng, emit, list(reads), list(writes))

    def dma(self, eng, sem, out, in_, reads=(), writes=()):
        def emit(e):
            return e.dma_start(out=out, in_=in_)
        return self._add(eng, emit, list(reads), list(writes), dma_sem=sem)

    def finish(self):
        nc = self.nc
        with ExitStack() as st:
            esem = {e: st.enter_context(nc.semaphore("s_" + e)) for e in ENGS}
            dsem = {k: st.enter_context(nc.semaphore("d_%d" % i)) for i, k in enumerate(self.dma_keys)}
            semval = {}
            for e in ENGS:
                c = 0
                vals = []
                for o in self.ops[e]:
                    if o.need_inc:
                        c += 1
                    vals.append(c)
                semval[e] = vals
            block = st.enter_context(nc.Block())

            def run(eng_name, e, final=False):
                for o in self.ops[eng_name]:
                    for w in o.waits:
                        if w[0] == "e":
                            e.wait_ge(esem[w[1]], semval[w[1]][w[2]])
                        else:
                            e.wait_ge(dsem[w[1]], w[2])
                    ins = o.emit(e)
                    if o.dma_sem is not None:
                        ins.then_inc(dsem[o.dma_sem], 16)
                    elif o.need_inc:
                        ins.then_inc(esem[eng_name], 1)
                if final:
                    for k in self.dma_keys:
                        e.wait_ge(dsem[k], self.dma_count[k])

            @block.tensor
            def _(e):
                run(PE, e)

            @block.scalar
            def _(e):
                run(ACT, e)

            @block.vector
            def _(e):
                run(DVE, e)

            @block.gpsimd
            def _(e):
                run(POOL, e)

            @block.sync
            def _(e):
                run(SP, e, final=True)


def qperm(h):
    hkv, g = divmod(h, 4)
    return (hkv // 2) * 4 + g, hkv % 2


STOP = [99]


def build_program(n_prompt_tiles=16, do_sample=True):
    nc = bass.Bass("TRN2", target_bir_lowering=False)
    dt_in = lambda name, shape: nc.dram_tensor(name, list(shape), F32, kind="ExternalInput").ap()
    dt_out = lambda name, shape: nc.dram_tensor(name, list(shape), F32, kind="ExternalOutput").ap()
    xp = dt_in("xp", [2, SEQ, D])
    xs = dt_in("xs", [32, D])
    ck = dt_in("ck", [2, 128, 256])
    cv = dt_in("cv", [2, 128, 256])
    g_in = [dt_in("g%d" % i, [1, D]) for i in range(4)]
    w_in = dt_in("w_in", [D, 5632])
    sinks = dt_in("sinks", [1, 16])
    ln_g = dt_in("ln_g", [1, D])
    ln_b = dt_in("ln_b", [1, D])
    w_s = dt_in("w_s", [4, 128, 128])
    b_s = dt_in("b_s", [1, 512])
    wap = dt_in("wap", [D, D])
    wgp = dt_in("wgp", [D, D])
    wout = dt_in("wout", [D, D])
    wfi = dt_in("wfi", [D, 5632])
    wfo = dt_in("wfo", [DFF, D])
    yp = dt_out("yp", [2, SEQ, D])
    ys = dt_out("ys", [32, D])
    kp = dt_out("kp", [2, 128, 256])
    vp = dt_out("vp", [2, 128, 256])
    ks = dt_out("ks", [2, 128, 256])
    vs = dt_out("vs", [2, 128, 256])
    gs = dt_out("gs", [32, D])
    sc = lambda name, shape: nc.dram_tensor(name, list(shape), BF16, kind="Internal").ap()
    s_win = sc("s_win", [128, 11, 8, 512])
    s_wap = sc("s_wap", [128, 2, 8, 512])
    s_wgp = sc("s_wgp", [128, 2, 8, 512])
    s_wout = sc("s_wout", [128, 2, 8, 512])
    s_wfi = sc("s_wfi", [128, 11, 8, 512])
    s_wfo = sc("s_wfo", [128, 2, 22, 512])

    with ExitStack() as st:
        sb = lambda name, shape, dt: st.enter_context(nc.sbuf_tensor(name, list(shape), dt))
        m = MK(nc)
        NRING = 8
        ring = sb("ring", [128, NRING, 8, 512], BF16)
        xt = sb("xt", [128, 6, D], F32)
        hT = sb("hT", [128, 8, 512], BF16)
        big = sb("big", [128, 24, 512], BF16)
        qT = big[:, 0:8, :]
        uT = big[:, 8:16, :]
        sgaT = big[:, 16:24, :]
        actT = big[:, 0:22, :]
        sgbT = sb("sgbT", [128, 8, 512], BF16)
        aT = sb("aT", [128, 8, 512], BF16)
        mgT = sb("mgT", [128, 8, 512], BF16)
        kT = sb("kT", [128, 2, 640], BF16)
        Vt = sb("Vt", [128, 5, 256], BF16)
        gvf = sb("gvf", [128, D], F32)
        gvn = sb("gvn", [128, 1, D], BF16)
        PT = sb("PT", [128, 8, 512], BF16)
        dent = sb("dent", [128, 2, 512], F32)
        junk = dent[:, 0, :].bitcast(BF16).rearrange("p (a b) -> p a b", a=2)
        hst = gvn
        tmpf = sb("tmpf", [128, 4, 512], F32)
        kvo = tmpf[:, 0, :]
        stt = sb("stt", [128, 64], F32)
        gS = sb("gS", [128, 4, D], F32)
        lng = sb("lng", [128, D], F32)
        lnb = sb("lnb", [128, D], F32)
        es = sb("es", [128, 16], F32)
        es2 = sb("es2", [128, 8], F32)
        negh = sb("negh", [128, 4], F32)
        ident = sb("ident", [128, 128], BF16)
        ones = sb("ones", [128, 128], BF16)
        WmT = sb("WmT", [128, 4, 128], BF16)
        WmT16 = sb("WmT16", [32, 4, 32], BF16)
        wsf = sb("wsf", [128, 128], F32)
        wsb = sb("wsb", [128, 128], BF16)
        bsb = sb("bsb", [1, 4, 128], BF16)
        bs16 = sb("bs16", [1, 4, 32], BF16)
        msk = sb("msk", [32, 2, 64], BF16)
        kTc = sb("kTc", [128, 2, 2, 128], BF16)
        Vc = sb("Vc", [128, 2, 256], BF16)
        ckf = sb("ckf", [128, 256], F32)
        ckb = sb("ckb", [128, 256], BF16)
        banks = [st.enter_context(nc.psum_tensor("ps%d" % i, [128, 512], F32)) for i in range(8)]
        bank_ctr = [0]

        kvst_ctr = [0]

        def kvsem():
            kvst_ctr[0] += 1
            return ("st_kv", kvst_ctr[0])

        def nb():
            i = 1 + bank_ctr[0] % 7
            bank_ctr[0] += 1
            return i

        def mm(b, out, lhsT, rhs, start, stop, reads):
            m.op(PE, lambda e: e.matmul(out, lhsT=lhsT, rhs=rhs, start=start, stop=stop), reads=reads,
                 writes=[("ps", b)])

        def act(out, in_, func, reads, writes, scale=1.0, accum_out=None):
            if accum_out is None:
                m.op(ACT, lambda e: e.activation(out=out, in_=in_, func=func, scale=scale), reads=reads, writes=writes)
            else:
                m.op(ACT, lambda e: e.activation(out=out, in_=in_, func=func, scale=scale, accum_out=accum_out),
                     reads=reads, writes=writes)

        def tcopy(eng, out, in_, reads, writes):
            if eng == ACT:
                m.op(eng, lambda e: e.activation(out=out, in_=in_, func=AF.Copy), reads=reads, writes=writes)
            else:
                m.op(eng, lambda e: e.tensor_copy(out=out, in_=in_), reads=reads, writes=writes)

        def tt(eng, out, in0, in1, op, reads, writes):
            m.op(eng, lambda e: e.tensor_tensor(out=out, in0=in0, in1=in1, op=op), reads=reads, writes=writes)

        def ts(eng, out, in0, s1, s2, op0, op1, reads, writes):
            if s2 is None:
                m.op(eng, lambda e: e.tensor_scalar(out=out, in0=in0, scalar1=s1, scalar2=None, op0=op0),
                     reads=reads, writes=writes)
            else:
                m.op(eng, lambda e: e.tensor_scalar(out=out, in0=in0, scalar1=s1, scalar2=s2, op0=op0, op1=op1),
                     reads=reads, writes=writes)

        def stt_op(eng, out, in0, scalar, in1, op0, op1, reads, writes):
            m.op(eng, lambda e: e.scalar_tensor_tensor(out=out, in0=in0, scalar=scalar, in1=in1, op0=op0, op1=op1),
                 reads=reads, writes=writes)

        def wsrc(w, c0, n):
            return w[:, c0:c0 + n].rearrange("(kc p) n -> p kc n", p=128)

        conv_jobs = {}
        blk_view = {}

        def cj(key, sem, idx, in_):
            conv_jobs.setdefault(key, []).append((sem, idx, in_))

        A_ = slice(None)
        for cb in range(11):
            blk_view[("s_win", cb)] = s_win[:, cb, :, :]
            blk_view[("s_wfi", cb)] = s_wfi[:, cb, :, :]
        for cb in range(2):
            blk_view[("s_wap", cb)] = s_wap[:, cb, :, :]
            blk_view[("s_wgp", cb)] = s_wgp[:, cb, :, :]
            blk_view[("s_wout", cb)] = s_wout[:, cb, :, :]
        for h in range(16):
            t, half = qperm(h)
            cb, j = divmod(t, 4)
            c0 = j * 128 + half * 64
            cj(("s_win", cb), ("cv_win", cb), (A_, A_, slice(c0, c0 + 64)), wsrc(w_in, h * 64, 64))
        for cb in range(2, 11):
            cj(("s_win", cb), ("cv_win", cb), (A_, A_, A_), wsrc(w_in, cb * 512, 512))
        for h in range(16):
            t, half = qperm(h)
            for cb in range(2):
                cj(("s_wap", cb), ("cv_wap", cb), (slice(half * 64, half * 64 + 64), t, A_),
                   wap[h * 64:(h + 1) * 64, cb * 512:(cb + 1) * 512])
        for cb in range(2):
            cj(("s_wgp", cb), ("cv_wgp", cb), (A_, A_, A_), wsrc(wgp, cb * 512, 512))
        for cb in range(2):
            cj(("s_wout", cb), ("cv_wout", cb), (A_, A_, A_), wsrc(wout, cb * 512, 512))
        for i in range(11):
            cj(("s_wfi", i), ("cv_wfi", i), (A_, A_, slice(0, 256)), wsrc(wfi, 256 * i, 256))
            cj(("s_wfi", i), ("cv_wfi", i), (A_, A_, slice(256, 512)), wsrc(wfi, DFF + 256 * i, 256))
        for n in range(2):
            for c, (k0, k1) in enumerate(((0, 8), (8, 16), (16, 22))):
                blk_view[("s_wfo", n, c)] = s_wfo[:, n, k0:k1, :]
                cj(("s_wfo", n, c), ("cv_wfo", n, c), (A_, A_, A_),
                   wfo[k0 * 128:k1 * 128, n * 512:(n + 1) * 512].rearrange("(kc p) n -> p kc n", p=128))
        conv_order = ([("s_win", c) for c in (0, 1, 2, 3, 4, 7, 5, 6, 8, 9, 10)]
                      + [("s_wap", 0), ("s_wgp", 0), ("s_wap", 1), ("s_wgp", 1), ("s_wout", 0), ("s_wout", 1)]
                      + [("s_wfi", i) for i in range(11)]
                      + [("s_wfo", n, c) for n in range(2) for c in range(3)])
        conv_ptr = [0]
        CONV_LOOKAHEAD = [14]

        def conv_upto(key):
            if conv_ptr[0] >= len(conv_order):
                return
            tgt = min(len(conv_order), conv_order.index(key) + 1 + CONV_LOOKAHEAD[0])
            while conv_ptr[0] < tgt:
                k = conv_order[conv_ptr[0]]
                for sem, idx, in_ in conv_jobs[k]:
                    m.dma(POOL, sem, blk_view[k][idx], in_, writes=[k])
                conv_ptr[0] += 1

        x_preloaded = [False]
        if n_prompt_tiles > 0:
            for b in range(4):
                m.dma(SP, ("xld", b), xt[:, b, :], xp[0, b * 128:(b + 1) * 128, :], writes=[("xt", b)])
            x_preloaded[0] = True

        for i in range(4):
            m.dma(SP, ("c_g", i), gS[:, i, :], g_in[i][0:1, :].partition_broadcast(128), writes=[("gS", i)])
            act(gS[:, i, :], gS[:, i, :], AF.Copy, [("gS", i)], [("gS", i)], scale=32.0)
        m.dma(SP, "c_lng", lng[:], ln_g[0:1, :].partition_broadcast(128), writes=["lng"])
        m.dma(SP, "c_lnb", lnb[:], ln_b[0:1, :].partition_broadcast(128), writes=["lnb"])
        m.dma(SP, "c_es", es[:], sinks[0:1, :].partition_broadcast(128), writes=["es"])
        act(es[:], es[:], AF.Exp, ["es"], ["es"])
        for tp_ in range(2):
            for hf in range(2):
                tcopy(DVE, es2[64 * hf:64 * hf + 64, 4 * tp_:4 * tp_ + 4],
                      es[64 * hf:64 * hf + 64, 4 * (2 * tp_ + hf):4 * (2 * tp_ + hf) + 4], ["es"], ["es2"])
        m.op(POOL, lambda e: e.memset(negh[:], -0.5), writes=["negh"])
        m.op(POOL, lambda e: e.memset(ones[:], 1.0), writes=["ones"])
        m.op(POOL, lambda e: e.memset(wsf[:], 1.0), writes=["wsf"])
        m.op(POOL, lambda e: e.affine_select(out=wsf[:], in_=wsf[:], pattern=[[-1, 128]],
                                             compare_op=ALU.is_equal, fill=0.0, base=0, channel_multiplier=1),
             reads=["wsf"], writes=["wsf"])
        tcopy(DVE, ident[:], wsf[:], ["wsf"], ["ident"])
        m.op(POOL, lambda e: e.memset(PT[:], 0.0), writes=[("PT", i) for i in range(8)])
        pT_b = 0
        pTv = banks[pT_b][:].bitcast(BF16)
        def setup_wmt():
            for g in range(4):
                m.dma(SP, "c_ws", wsf[:], w_s[g, :, :], writes=["wsf"])
                m.op(POOL, lambda e: e.affine_select(out=wsf[:], in_=wsf[:], pattern=[[-1, 128]], compare_op=ALU.is_ge,
                                                     fill=0.0, base=0, channel_multiplier=1),
                     reads=["wsf"], writes=["wsf"])
                tcopy(DVE, wsb[:], wsf[:], ["wsf"], ["wsb"])
                m.op(PE, lambda e: e.transpose(out=pTv[:, 0:128], in_=wsb[:], identity=ident[:]),
                     reads=["wsb", "ident"], writes=[("ps", pT_b)])
                tcopy(DVE, WmT[:, g, :], pTv[:, 0:128], [("ps", pT_b)], ["WmT"])
            m.dma(SP, "c_bs", kvo[0:1, :], b_s[0:1, :], writes=[("tmpf", 0)])
            tcopy(DVE, bsb[:].rearrange("p g t -> p (g t)"), kvo[0:1, :], [("tmpf", 0)], ["bsb"])

        def setup_sample_consts():
            if True:
                for g in range(4):
                    m.op(POOL, lambda e: e.memset(wsf[0:32, 0:32], 0.0), writes=["wsf"])
                    for s in range(2):
                        m.dma(SP, "c_ws", wsf[16 * s:16 * s + 16, 16 * s:16 * s + 16], w_s[g, 0:16, 0:16], reads=[],
                              writes=["wsf"])
                    m.op(POOL, lambda e: e.affine_select(out=wsf[0:32, 0:32], in_=wsf[0:32, 0:32], pattern=[[-1, 32]],
                                                         compare_op=ALU.is_ge, fill=0.0, base=0, channel_multiplier=1),
                         reads=["wsf"], writes=["wsf"])
                    tcopy(DVE, wsb[0:32, 0:32], wsf[0:32, 0:32], ["wsf"], ["wsb"])
                    m.op(PE, lambda e: e.transpose(out=pTv[0:32, 0:32], in_=wsb[0:32, 0:32], identity=ident[0:32, 0:32]),
                         reads=["wsb", "ident"], writes=[("ps", pT_b)])
                    tcopy(DVE, WmT16[:, g, :], pTv[0:32, 0:32], [("ps", pT_b)], ["WmT16"])
                for s in range(2):
                    tcopy(DVE, bs16[:, :, 16 * s:16 * s + 16], bsb[:, :, 0:16], ["bsb"], ["bs16"])
                m.op(POOL, lambda e: e.memset(msk[:], 1.0), writes=["msk"])
                m.op(POOL, lambda e: e.affine_select(out=msk[:, 0, :], in_=msk[:, 0, :], pattern=[[0, 64]],
                                                     compare_op=ALU.is_ge, fill=0.0, base=15, channel_multiplier=-1),
                     reads=["msk"], writes=["msk"])
                m.op(POOL, lambda e: e.affine_select(out=msk[:, 1, :], in_=msk[:, 1, :], pattern=[[0, 64]],
                                                     compare_op=ALU.is_ge, fill=0.0, base=-16, channel_multiplier=1),
                     reads=["msk"], writes=["msk"])


        ring_ctr = [0]

        FIRST_PASS = [False]

        def wload(src_ap, src_key, nkc=8):
            s = ring_ctr[0] % NRING
            ring_ctr[0] += 1
            if FIRST_PASS[0]:
                slot_v = ring[:, s, 0:nkc, :]
                for sem, idx, in_ in conv_jobs[src_key]:
                    m.dma(POOL, ("ring0", s), slot_v[idx], in_, writes=[("ring", s)])
                m.dma(SP, ("cvst", src_key), blk_view[src_key], slot_v, reads=[("ring", s)], writes=[src_key])
                return ring[:, s, :, :], ("ring", s)
            conv_upto(src_key)
            m.dma(SP, ("ring", s), ring[:, s, 0:nkc, :], src_ap, reads=[src_key], writes=[("ring", s)])
            return ring[:, s, :, :], ("ring", s)

        stat_ctr = [0]

        def stat(n=1):
            c = stat_ctr[0] % 64
            if c + n > 64:
                c = 0
            stat_ctr[0] = c + n
            return c

        tmp_ctr = [0]

        def ntmp():
            i = tmp_ctr[0] % 4
            tmp_ctr[0] += 1
            return i

        hst_ctr = [0]
        junk_ctr = [0]

        pre_done = [False]
        wmt_done = [False]
        sample_consts_done = [False]

        PX = [POOL]
        YQ = [POOL]

        def rsqrt_op(P, out_ap, in_ap, rkeys, wkeys):
            if PX[0] == POOL:
                tt(POOL, out_ap, in_ap, negh[0:P, 0:1], ALU.pow, rkeys + ["negh"], wkeys)
            else:
                act(out_ap, in_ap, AF.Sqrt, rkeys, wkeys)
                m.op(DVE, lambda e: e.reciprocal(out=out_ap, in_=out_ap), reads=wkeys, writes=wkeys)

        def xkeys(xslot):
            return [("xt", xslot), ("xth", xslot, 0), ("xth", xslot, 1)]

        def norm_stats(P, xslot):
            c = stat(1)
            act(junk[0:P, :, :].rearrange("p a b -> p (a b)"), xt[0:P, xslot, :], AF.Square,
                xkeys(xslot), [("st", c), ("dent", 0)], accum_out=stt[0:P, c:c + 1])
            c2 = stat(2)
            ts(DVE, stt[0:P, c2:c2 + 1], stt[0:P, c:c + 1], float(D * EPS), None, ALU.add, None,
               [("st", c)], [("st", c2)])
            rsqrt_op(P, stt[0:P, c2 + 1:c2 + 2], stt[0:P, c2:c2 + 1], [("st", c2)], [("st", c2 + 1)])
            return c2 + 1

        HKEYS = [("gvn", 0, 0), ("gvn", 0, 1)]

        def norm_scale(P, gi, xslot, r):
            stt_op(DVE, hst[0:P, 0, :], xt[0:P, xslot, :], stt[0:P, r:r + 1], gS[0:P, gi, :], ALU.mult, ALU.mult,
                   xkeys(xslot) + [("st", r), ("gS", gi)], HKEYS)

        def norm_T(P, b):
            for kc in range(8):
                m.op(PE, (lambda kc: lambda e: e.transpose(out=pTv[:, kc * P:(kc + 1) * P],
                                                           in_=hst[0:P, 0, kc * 128:(kc + 1) * 128],
                                                           identity=ident[0:P, 0:P]))(kc),
                     reads=HKEYS + ["ident"], writes=[("ps", pT_b)])
            act(hT[:, :, b * P:(b + 1) * P], pTv[:, 0:8 * P].rearrange("p (k t) -> p k t", k=8), AF.Copy,
                [("ps", pT_b)], [("hT", b)])

        def norm_apply(P, b, gi, xslot, r):
            norm_scale(P, gi, xslot, r)
            norm_T(P, b)

        def norm_transpose(P, b, gi, xslot):
            norm_apply(P, b, gi, xslot, norm_stats(P, xslot))

        def xsrc(sample, seq, t0, b):
            return xs[:, :] if sample else xp[seq, t0 + b * 128:t0 + (b + 1) * 128, :]

        def run_pass(sample, seq=0, t0=0, xslots=(0,), nxt=None):
            P = 32 if sample else 128
            NB = 1 if sample else 4
            N = P * NB
            first_block_of_seq = (t0 == 0)
            last_tile_of_seq = (t0 + 512 == SEQ)
            hkeys = [("hT", b) for b in range(NB)]

            m.stage = "norm1"
            if not pre_done[0]:
                if x_preloaded[0] and not sample:
                    x_preloaded[0] = False
                else:
                    for b in range(NB):
                        m.dma(SP, ("xld", xslots[b]), xt[0:P, xslots[b], :], xsrc(sample, seq, t0, b),
                              writes=[("xt", xslots[b])])
                for b in range(NB):
                    norm_transpose(P, b, 0, xslots[b])
            pre_done[0] = False

            m.stage = "A.q"

            loop_banks = [None]
            lb_ctr = [0]
            S_BANKS = (1, 2, 6)
            sb_ctr = [0]

            def nbl():
                if loop_banks[0] is None:
                    return nb()
                i = loop_banks[0][lb_ctr[0] % 2]
                lb_ctr[0] += 1
                return i

            def ws_tile(wv, wkey, c0, dst, dkey, func, eng_copy=None, scale=1.0):
                b_ = nbl()
                for kc in range(8):
                    mm(b_, banks[b_][:, 0:N], wv[:, kc, c0:c0 + 128], hT[:, kc, 0:N], kc == 0, kc == 7,
                       [wkey] + hkeys)
                if eng_copy is not None:
                    tcopy(eng_copy, dst, banks[b_][:, 0:N], [("ps", b_)], [dkey])
                else:
                    act(dst, banks[b_][:, 0:N], func, [("ps", b_)], [dkey], scale=scale)

            for cb in range(2):
                wv, wk = wload(s_win[:, cb, :, :], ("s_win", cb))
                for j in range(4):
                    ws_tile(wv, wk, 128 * j, qT[:, 4 * cb + j, 0:N], ("qT", 4 * cb + j), AF.Copy)
            m.stage = "A.kv"
            wv, wk = wload(s_win[:, 2, :, :], ("s_win", 2))
            for j in range(2):
                if sample:
                    ws_tile(wv, wk, 128 * j, kT[:, j, 0:32], ("kT", 0), None, eng_copy=DVE)
                else:
                    ws_tile(wv, wk, 128 * j, kT[:, j, 128:640], ("kTw", j), None, eng_copy=DVE)
            if not sample:
                for sl in range(1, 5):
                    m.last_w[("kT", sl)] = m.last_w[("kTw", 1)]
                    m.readers[("kT", sl)] = []
            for b in range(NB):
                b_ = nb()
                need_k = (not sample) and last_tile_of_seq and b == NB - 1
                c0 = 0 if need_k else 256
                for kc in range(8):
                    mm(b_, banks[b_][0:P, c0:512], hT[:, kc, b * P:(b + 1) * P], wv[:, kc, c0:512], kc == 0, kc == 7,
                       [wk, ("hT", b)])
                slot = 0 if sample else b + 1
                tcopy(DVE, Vt[0:P, slot, :], banks[b_][0:P, 256:512], [("ps", b_)], [("Vt", slot)])
                if sample:
                    for s in range(2):
                        bq = nb()
                        for kc in range(8):
                            mm(bq, banks[bq][0:16, :], hT[:, kc, 16 * s:16 * s + 16], wv[:, kc, :], kc == 0, kc == 7,
                               [wk, ("hT", b)])
                        tcopy(ACT, kvo[0:16, :], banks[bq][0:16, :], [("ps", bq)], [("tmpf", 0)])
                        m.dma(POOL, kvsem(), ks[s, 112:128, :], kvo[0:16, 0:256], reads=[("tmpf", 0)])
                        m.dma(POOL, kvsem(), vs[s, 112:128, :], kvo[0:16, 256:512], reads=[("tmpf", 0)])
                        for src_c, dst_c in ((ck, ks), (cv, vs)):
                            m.dma(SP, "c_ck", ckf[0:112, :], src_c[s, 16:128, :], writes=["ckf"])
                            m.dma(POOL, kvsem(), dst_c[s, 0:112, :], ckf[0:112, :], reads=["ckf"])
                elif last_tile_of_seq and b == NB - 1:
                    tcopy(ACT, kvo[:, :], banks[b_][:, :], [("ps", b_)], [("tmpf", 0)])
                    m.dma(POOL, kvsem(), kp[seq, :, :], kvo[:, 0:256], reads=[("tmpf", 0)])
                    m.dma(POOL, kvsem(), vp[seq, :, :], kvo[:, 256:512], reads=[("tmpf", 0)])
            m.stage = "A.u"
            for cb in (3, 4):
                wv, wk = wload(s_win[:, cb, :, :], ("s_win", cb))
                for j in range(4):
                    f = 4 * (cb - 3) + j
                    ws_tile(wv, wk, 128 * j, uT[:, f, 0:N], ("uT", f), AF.Gelu_apprx_tanh)
            if not wmt_done[0]:
                m.stage = "setup"
                setup_wmt()
                wmt_done[0] = True
            ukeys = [("uT", j) for j in range(8)]
            qkeys = [("qT", j) for j in range(8)]
            akeys = [("aT", j) for j in range(8)]

            gvw = {}

            def gv_mm(b):
                m.stage = "A.gv"
                if not gvw:
                    gvw[5] = wload(s_win[:, 5, :, :], ("s_win", 5))
                    gvw[6] = wload(s_win[:, 6, :, :], ("s_win", 6))
                for hh in range(2):
                    wv, wk = gvw[5 + hh]
                    b_ = nbl()
                    for kc in range(8):
                        mm(b_, banks[b_][0:P, :], hT[:, kc, b * P:(b + 1) * P], wv[:, kc, :], kc == 0, kc == 7,
                           [wk, ("hT", b)])
                    act(gvf[0:P, hh * 512:(hh + 1) * 512], banks[b_][0:P, :], AF.Gelu_apprx_tanh, [("ps", b_)],
                        [("gvf", hh)])
                c = stat(16)
                for hh in range(2):
                    m.op(DVE, (lambda hh, c: lambda e: e.bn_stats(out=stt[0:P, c + 6 * hh:c + 6 * hh + 6],
                                                                  in_=gvf[0:P, hh * 512:(hh + 1) * 512]))(hh, c),
                         reads=[("gvf", hh)], writes=[("st", c + 6 * hh)])
                m.op(DVE, (lambda c: lambda e: e.bn_aggr(out=stt[0:P, c + 12:c + 14], in_=stt[0:P, c:c + 12]))(c),
                     reads=[("st", c), ("st", c + 6)], writes=[("st", c + 12)])
                ts(DVE, stt[0:P, c + 14:c + 15], stt[0:P, c + 13:c + 14], float(EPS), None, ALU.add, None,
                   [("st", c + 12)], [("st", c + 14)])
                rsqrt_op(P, stt[0:P, c + 15:c + 16], stt[0:P, c + 14:c + 15], [("st", c + 14)], [("st", c + 15)])
                gk = [("gvf", 0), ("gvf", 1)]
                gslot = 0
                for hh, eng in ((0, DVE), (1, PX[0])):
                    cs = slice(hh * 512, (hh + 1) * 512)
                    gkh = [("gvf", hh)]
                    ts(DVE, gvf[0:P, cs], gvf[0:P, cs], stt[0:P, c + 12:c + 13], stt[0:P, c + 15:c + 16],
                       ALU.subtract, ALU.mult, gkh + [("st", c + 12), ("st", c + 15)], gkh)
                    tt(eng, gvf[0:P, cs], gvf[0:P, cs], lng[0:P, cs], ALU.mult, gkh + ["lng"], gkh)
                    if sample:
                        tt(eng, gvf[0:P, cs], gvf[0:P, cs], lnb[0:P, cs], ALU.add, gkh + ["lnb"], gkh)
                    else:
                        tt(eng, gvn[0:P, gslot, cs], gvf[0:P, cs], lnb[0:P, cs], ALU.add, gkh + ["lnb"],
                           [("gvn", gslot, hh)])
                if sample:
                    m.dma(POOL, "st_gs", gs[:, :], gvf[0:P, :], reads=gk)
                    for hh in range(2):
                        cs = slice(hh * 512, (hh + 1) * 512)
                        tcopy(DVE, gvn[0:P, gslot, cs], gvf[0:P, cs], [("gvf", hh)], [("gvn", gslot, hh)])

            def spatial(b):
                m.stage = "A.sp"
                gslot = 0
                wmt = WmT16 if sample else WmT
                bsx = bs16 if sample else bsb
                for half in range(1 if sample else 2):
                    b_ = nbl()
                    jr = range(8) if sample else range(4 * half, 4 * half + 4)
                    for jj, j in enumerate(jr):
                        g = j // 2
                        o = banks[b_][:, jj * P:(jj + 1) * P]
                        mm(b_, o, gvn[0:P, gslot, j * 128:(j + 1) * 128], wmt[0:P, g, :], True, False,
                           [("gvn", gslot, j // 4), "WmT", "WmT16"])
                        mm(b_, o, ones[0:1, 0:128], bsx[0:1, g, :], False, True, ["ones", "bsb", "bs16"])
                    nj = len(jr)
                    j0 = jr[0]
                    tt(DVE, uT[:, j0:j0 + nj, b * P:(b + 1) * P],
                       banks[b_][:, 0:nj * P].rearrange("p (j t) -> p j t", j=nj),
                       uT[:, j0:j0 + nj, b * P:(b + 1) * P], ALU.mult,
                       [("ps", b_)] + ukeys[j0:j0 + nj], ukeys[j0:j0 + nj])

            gw = {}

            def gate_tile(n):
                m.stage = "A.gates"
                cb = 7 + n // 4
                if cb not in gw:
                    gw.clear()
                    gw[cb] = wload(s_win[:, cb, :, :], ("s_win", cb))
                wv, wk = gw[cb]
                if n < 8:
                    ws_tile(wv, wk, 128 * (n % 4), sgaT[:, n, 0:N], ("sga", n), AF.Tanh, scale=0.5)
                else:
                    ws_tile(wv, wk, 128 * (n % 4), sgbT[:, n - 8, 0:N], ("sgb", n - 8), AF.Tanh, scale=0.5)

            if sample:
                m.stage = "attn"
                for s in range(2):
                    m.dma(SP, "c_ck", ckf[:], ck[s, :, :], writes=["ckf"])
                    tcopy(DVE, ckb[:], ckf[:], ["ckf"], ["ckb"])
                    for j in range(2):
                        m.op(PE, (lambda j: lambda e: e.transpose(out=pTv[:, j * 128:(j + 1) * 128],
                                                                  in_=ckb[:, j * 128:(j + 1) * 128],
                                                                  identity=ident[:]))(j),
                             reads=["ckb", "ident"], writes=[("ps", pT_b)])
                    tcopy(DVE, kTc[:, s, :, :], pTv[:, 0:256].rearrange("p (j t) -> p j t", j=2), [("ps", pT_b)],
                          [("kTc", s)])
                    m.dma(SP, "c_ck", ckf[:], cv[s, :, :], writes=["ckf"])
                    tcopy(DVE, Vc[:, s, :], ckf[:], ["ckf"], [("Vc", s)])

            def attn_S(n, b, tp, half):
                m.stage = "attn"
                hs0 = half * 64
                pp = (n % 2) * 4 + half * 2
                first = (not sample) and first_block_of_seq and b == 0
                b1 = None
                if not first:
                    b1 = S_BANKS[sb_ctr[0] % 3]
                    sb_ctr[0] += 1
                b2 = S_BANKS[sb_ctr[0] % 3]
                sb_ctr[0] += 1
                if sample:
                    s_ = b
                    qv = qT[hs0:hs0 + 64, 4 * tp:4 * tp + 4, 16 * s_:16 * s_ + 16]
                    o1 = banks[b1][:, 0:64].rearrange("p (g q) -> p g q", g=4)
                    o2 = banks[b2][0:32, 0:64].rearrange("p (g q) -> p g q", g=4)
                    mm(b1, o1, kTc[hs0:hs0 + 64, s_, tp, :], qv, True, True, [("kTc", s_)] + qkeys)
                    mm(b2, o2, kT[hs0:hs0 + 64, tp, 0:32], qv, True, True, [("kT", 0)] + qkeys)
                    P1 = PT[:, pp, 0:64]
                    P2 = PT[0:32, pp + 1, 0:64]
                    act(P1, banks[b1][:, 0:64], AF.Exp, [("ps", b1)], [("PT", pp)], scale=0.125)
                    act(P2, banks[b2][0:32, 0:64], AF.Exp, [("ps", b2)], [("PT", pp + 1)], scale=0.125)
                    tt(DVE, P2, P2, msk[:, s_, :], ALU.mult, [("PT", pp + 1), "msk"], [("PT", pp + 1)])
                    return
                qv = qT[hs0:hs0 + 64, 4 * tp:4 * tp + 4, b * 128:(b + 1) * 128]
                P1 = PT[:, pp, :].rearrange("p (g q) -> p g q", g=4)
                P2 = PT[:, pp + 1, :].rearrange("p (g q) -> p g q", g=4)
                o1 = banks[b1][:, :].rearrange("p (g q) -> p g q", g=4) if b1 is not None else None
                o2 = banks[b2][:, :].rearrange("p (g q) -> p g q", g=4)
                if not first:
                    mm(b1, o1, kT[hs0:hs0 + 64, tp, b * 128:(b + 1) * 128], qv, True, True, [("kT", b)] + qkeys)
                mm(b2, o2, kT[hs0:hs0 + 64, tp, (b + 1) * 128:(b + 2) * 128], qv, True, True, [("kT", b + 1)] + qkeys)
                if not first:
                    act(P1[0:64, :, 0:64], o1[0:64, :, 0:64], AF.Exp, [("ps", b1)], [("PT", pp)], scale=0.125)
                    act(P1[64:128, :, :], o1[64:128, :, :], AF.Exp, [("ps", b1)], [("PT", pp)], scale=0.125)
                act(P2[0:64, :, :], o2[0:64, :, :], AF.Exp, [("ps", b2)], [("PT", pp + 1)], scale=0.125)
                act(P2[64:128, :, 64:128], o2[64:128, :, 64:128], AF.Exp, [("ps", b2)], [("PT", pp + 1)], scale=0.125)

            def attn_PV(n, b, tp):
                m.stage = "attn"
                d_i = n % 2
                bo, bd = (3 if n % 2 == 0 else 4), 5
                W = 64 if sample else 512
                NQ = 16 if sample else 128
                if sample:
                    for half in range(2):
                        hs0 = half * 64
                        pp = (n % 2) * 4 + half * 2
                        vc0 = 128 * tp + 64 * half
                        oo = banks[bo][hs0:hs0 + 64, 0:W]
                        od = banks[bd][hs0:hs0 + 64, 0:W]
                        s_ = b
                        P1 = PT[:, pp, 0:64]
                        P2 = PT[0:32, pp + 1, 0:64]
                        mm(bo, oo, Vc[:, s_, vc0:vc0 + 64], P1, True, False, [("Vc", s_), ("PT", pp)])
                        mm(bo, oo, Vt[0:32, 0, vc0:vc0 + 64], P2, False, True, [("Vt", 0), ("PT", pp + 1)])
                        mm(bd, od, ones[:, 0:64], P1, True, False, ["ones", ("PT", pp)])
                        mm(bd, od, ones[0:32, 0:64], P2, False, True, ["ones", ("PT", pp + 1)])
                else:
                    first = first_block_of_seq and b == 0
                    for bank_i, use_v in ((bo, True), (bd, False)):
                        for which in ((1, 2) if not first else (2,)):
                            for half in range(2):
                                hs0 = half * 64
                                pp = (n % 2) * 4 + half * 2 + (which - 1)
                                vc0 = 128 * tp + 64 * half
                                slot = b if which == 1 else b + 1
                                lhs = Vt[:, slot, vc0:vc0 + 64] if use_v else ones[:, 0:64]
                                rk = [("Vt", slot), ("PT", pp)] if use_v else ["ones", ("PT", pp)]
                                mm(bank_i, banks[bank_i][hs0:hs0 + 64, :], lhs, PT[:, pp, :],
                                   which == 1 or first, which == 2, rk)
                if sample:
                    a_out = aT[:, 4 * tp:4 * tp + 4, 16 * b:16 * b + 16]
                else:
                    a_out = aT[:, 4 * tp:4 * tp + 4, b * 128:(b + 1) * 128]
                dv = dent[:, d_i, 0:W]
                esb = es2[:, 4 * tp:4 * tp + 4].unsqueeze(2).to_broadcast([128, 4, NQ])
                tt(DVE, dv.rearrange("p (g q) -> p g q", g=4),
                   banks[bd][:, 0:W].rearrange("p (g q) -> p g q", g=4), esb, ALU.add,
                   [("ps", bd), "es2"], [("dent", d_i)])
                m.op(DVE, (lambda dv: lambda e: e.reciprocal(out=dv, in_=dv))(dv), reads=[("dent", d_i)],
                     writes=[("dent", d_i)])
                tt(DVE, a_out, banks[bo][:, 0:W].rearrange("p (g q) -> p g q", g=4),
                   dv.rearrange("p (g q) -> p g q", g=4), ALU.mult, [("ps", bo), ("dent", d_i)],
                   akeys[4 * tp:4 * tp + 4])

            pairs = [(b, tp) for b in range(2 if sample else NB) for tp in range(2)]
            NG = len(pairs)
            gpg = 16 // NG
            loop_banks[0] = (7, 0)
            attn_S(0, *pairs[0], 0)
            attn_S(0, *pairs[0], 1)
            for n, (b, tp) in enumerate(pairs):
                if n + 1 < NG:
                    attn_S(n + 1, *pairs[n + 1], 0)
                attn_PV(n, b, tp)
                if n + 1 < NG:
                    attn_S(n + 1, *pairs[n + 1], 1)
                for t in range(gpg):
                    gate_tile(gpg * n + t)
                if sample:
                    if n == 2:
                        spatial(0)
                    if n == 0:
                        gv_mm(0)
                else:
                    if n % 2 == 0 and n >= 2:
                        spatial(n // 2 - 1)
                    if n % 2 == 0:
                        gv_mm(n // 2)
            if not sample:
                spatial(NB - 1)
            loop_banks[0] = None
            if not sample:
                m.stage = "attn"
                tcopy(PX[0], kT[:, :, 0:128], kT[:, :, 512:640], [("kT", 4)], [("kT", 0)])
                tcopy(PX[0], Vt[:, 0, :], Vt[:, 4, :], [("Vt", 4)], [("Vt", 0)])
                m.fence([DVE], [("kT", sl) for sl in range(1, 5)])

            m.stage = "merge"
            for cbm in range(2):
                wva, wka = wload(s_wap[:, cbm, :, :], ("s_wap", cbm))
                wvg, wkg = wload(s_wgp[:, cbm, :, :], ("s_wgp", cbm))
                for j in range(4):
                    f = 4 * cbm + j
                    ba, bb = nb(), nb()
                    for kc in range(8):
                        mm(ba, banks[ba][:, 0:N], wva[:, kc, j * 128:(j + 1) * 128], aT[:, kc, 0:N], kc == 0, kc == 7,
                           [wka] + akeys)
                    for kc in range(8):
                        mm(bb, banks[bb][:, 0:N], wvg[:, kc, j * 128:(j + 1) * 128], uT[:, kc, 0:N], kc == 0, kc == 7,
                           [wkg] + ukeys)
                    t1, t2 = ntmp(), ntmp()
                    stt_op(DVE, tmpf[:, t1, 0:N], sgaT[:, f, 0:N], 1.0, banks[ba][:, 0:N], ALU.add, ALU.mult,
                           [("ps", ba), ("sga", f)], [("tmpf", t1)])
                    stt_op(DVE, tmpf[:, t2, 0:N], sgbT[:, f, 0:N], 1.0, banks[bb][:, 0:N], ALU.add, ALU.mult,
                           [("ps", bb), ("sgb", f)], [("tmpf", t2)])
                    tt(DVE, mgT[:, f, 0:N], tmpf[:, t1, 0:N], tmpf[:, t2, 0:N], ALU.add,
                       [("tmpf", t1), ("tmpf", t2)], [("mgT", f)])
            mkeys = [("mgT", f) for f in range(8)]

            def post_norm_residual(b, bks, gi, xslot, eps_mul=1.0, store_dst=None):
                c = stat(2)
                for n in range(2):
                    jk = junk_ctr[0] % 2
                    junk_ctr[0] += 1
                    act(junk[0:P, jk, :], banks[bks[n]][0:P, :], AF.Square, [("ps", bks[n])],
                        [("st", c + n), ("dent", 0)], accum_out=stt[0:P, c + n:c + n + 1])
                c2 = stat(2)
                ts(DVE, stt[0:P, c2:c2 + 1], stt[0:P, c:c + 1], stt[0:P, c + 1:c + 2], float(D * EPS * eps_mul),
                   ALU.add, ALU.add, [("st", c), ("st", c + 1)], [("st", c2)])
                rsqrt_op(P, stt[0:P, c2 + 1:c2 + 2], stt[0:P, c2:c2 + 1], [("st", c2)], [("st", c2 + 1)])
                for n in range(2):
                    t1 = ntmp()
                    stt_op(DVE, tmpf[0:P, t1, :], banks[bks[n]][0:P, :], stt[0:P, c2 + 1:c2 + 2],
                           gS[0:P, gi, n * 512:(n + 1) * 512], ALU.mult, ALU.mult,
                           [("ps", bks[n]), ("st", c2 + 1), ("gS", gi)], [("tmpf", t1)])
                    if store_dst is None:
                        tt(PX[0] if n == 0 else DVE, xt[0:P, xslot, n * 512:(n + 1) * 512],
                           xt[0:P, xslot, n * 512:(n + 1) * 512], tmpf[0:P, t1, :], ALU.add,
                           [("tmpf", t1), ("xt", xslot), ("xth", xslot, n)], [("xth", xslot, n)])
                    else:
                        tt(PX[0] if n == 0 else DVE, tmpf[0:P, t1, :], xt[0:P, xslot, n * 512:(n + 1) * 512],
                           tmpf[0:P, t1, :], ALU.add, [("tmpf", t1), ("xt", xslot), ("xth", xslot, n)],
                           [("tmpf", t1)])
                        m.dma(YQ[0], ("yst", t1, YQ[0]), store_dst[:, n * 512:(n + 1) * 512], tmpf[0:P, t1, :],
                              reads=[("tmpf", t1)])

            m.stage = "wout"
            wvo = [wload(s_wout[:, n, :, :], ("s_wout", n)) for n in range(2)]
            wbks = []
            r3 = {}

            def pn(b):
                m.stage = "wout"
                post_norm_residual(b, wbks[b], 1, xslots[b], eps_mul=4.0)

            def s3(b):
                m.stage = "norm3"
                r3[b] = norm_stats(P, xslots[b])

            def a3(b):
                m.stage = "norm3"
                norm_apply(P, b, 2, xslots[b], r3[b])

            def wout_mm(b):
                m.stage = "wout"
                bks = [nb(), nb()]
                wbks.append(bks)
                for n in range(2):
                    for kc in range(8):
                        mm(bks[n], banks[bks[n]][0:P, :], mgT[:, kc, b * P:(b + 1) * P], wvo[n][0][:, kc, :], kc == 0,
                           kc == 7, [wvo[n][1]] + mkeys)

            if NB == 1:
                wout_mm(0); pn(0); s3(0); a3(0)
            else:
                def a3s(b):
                    m.stage = "norm3"
                    norm_scale(P, 2, xslots[b], r3[b])

                def a3t(b):
                    m.stage = "norm3"
                    norm_T(P, b)

                wout_mm(0); wout_mm(1); pn(0)
                wout_mm(2); pn(1); s3(0); pn(2)
                wout_mm(3); a3s(0); s3(1); pn(3)
                a3t(0); a3s(1); s3(2)
                a3t(1); a3s(2); s3(3)
                a3t(2); a3s(3)
                a3t(3)

            m.stage = "ffn_in"
            if nxt is not None:
                for bb_ in range(2):
                    m.dma(SP, ("xld", nxt["xslots"][bb_]), xt[:, nxt["xslots"][bb_], :],
                          xsrc(False, nxt["seq"], nxt["t0"], bb_), writes=[("xt", nxt["xslots"][bb_])])
            m.fence([DVE], qkeys + ukeys + [("sga", j) for j in range(8)])
            for i in range(11):
                wv, wk = wload(s_wfi[:, i, :, :], ("s_wfi", i))
                for jj in range(2):
                    bg, bu = nb(), nb()
                    for kc in range(8):
                        mm(bg, banks[bg][:, 0:N], wv[:, kc, jj * 128:(jj + 1) * 128], hT[:, kc, 0:N], kc == 0, kc == 7,
                           [wk] + hkeys)
                    for kc in range(8):
                        mm(bu, banks[bu][:, 0:N], wv[:, kc, 256 + jj * 128:256 + (jj + 1) * 128], hT[:, kc, 0:N],
                           kc == 0, kc == 7, [wk] + hkeys)
                    t1 = ntmp()
                    act(tmpf[:, t1, 0:N], banks[bg][:, 0:N], AF.Silu, [("ps", bg)], [("tmpf", t1)])
                    tt(DVE, actT[:, 2 * i + jj, 0:N], banks[bu][:, 0:N], tmpf[:, t1, 0:N], ALU.mult,
                       [("ps", bu), ("tmpf", t1)], [("actT", 2 * i + jj)])
            fkeys = [("actT", j) for j in range(22)]

            m.stage = "ffn_out"
            chunks = ((0, 8), (8, 16), (16, 22))
            wfo_v = [[wload(s_wfo[:, n, k0:k1, :], ("s_wfo", n, c), nkc=k1 - k0) for c, (k0, k1) in enumerate(chunks)]
                     for n in range(2)]
            rn = {}
            for b in range(NB):
                m.stage = "ffn_out"
                if nxt is not None and 1 <= b <= 2:
                    m.dma(SP, ("xld", nxt["xslots"][b + 1]), xt[:, nxt["xslots"][b + 1], :],
                          xsrc(False, nxt["seq"], nxt["t0"], b + 1), writes=[("xt", nxt["xslots"][b + 1])])
                bks = [nb(), nb()]
                for n in range(2):
                    for c, (k0, k1) in enumerate(chunks):
                        wv, wk = wfo_v[n][c]
                        for kc in range(k0, k1):
                            mm(bks[n], banks[bks[n]][0:P, :], actT[:, kc, b * P:(b + 1) * P], wv[:, kc - k0, :],
                               kc == 0, kc == 21, [wk] + fkeys)
                if nxt is not None:
                    m.stage = "norm1"
                    if b == 0:
                        rn[0] = norm_stats(128, nxt["xslots"][0])
                        rn[1] = norm_stats(128, nxt["xslots"][1])
                    norm_scale(128, 0, nxt["xslots"][b], rn[b])
                    norm_T(128, b)
                    if 1 <= b <= 2:
                        rn[b + 1] = norm_stats(128, nxt["xslots"][b + 1])
                    m.stage = "ffn_out"
                dst = ys[:, :] if sample else yp[seq, t0 + b * 128:t0 + (b + 1) * 128, :]
                post_norm_residual(b, bks, 3, xslots[b], store_dst=dst)
            m.fence([ACT, DVE], fkeys)
            if nxt is not None:
                pre_done[0] = True

        tiles = [dict(seq=i // 8, t0=(i % 8) * 512, xslots=[(4 * i + b) % 6 for b in range(4)])
                 for i in range(n_prompt_tiles)]
        for i, t in enumerate(tiles):
            nxt = tiles[i + 1] if i + 1 < len(tiles) else None
            if i == 0:
                PX[0], YQ[0], FIRST_PASS[0] = DVE, ACT, True
            if do_sample and i == len(tiles) - 1 and len(tiles) > 1:
                m.stage = "setup"
                setup_sample_consts()
                sample_consts_done[0] = True
            run_pass(False, seq=t["seq"], t0=t["t0"], xslots=t["xslots"], nxt=nxt)
            if i == 0:
                conv_ptr[0] = len(conv_order)
            PX[0], YQ[0], FIRST_PASS[0] = POOL, POOL, False
        if do_sample:
            if not sample_consts_done[0]:
                m.stage = "setup"
                setup_sample_consts()
            run_pass(True, xslots=(0,))
        m.finish()
        nc._mk_tags = {e: [o.tag for o in m.ops[e]] for e in ENGS}
    return nc


_NC_CACHE = {}


def _get_nc():
    if "nc" not in _NC_CACHE:
        _NC_CACHE["nc"] = build_program()
    return _NC_CACHE["nc"]


def make_in_maps(inputs):
    f = lambda a: np.ascontiguousarray(np.asarray(a, dtype=np.float32))
    x_prompt = f(inputs["x_prompt"])
    x_sample = f(inputs["x_sample"])
    ck = f(inputs["cache_attn_k"])[0].reshape(16, 128, 256)
    cv = f(inputs["cache_attn_v"])[0].reshape(16, 128, 256)
    shared = {
        "g0": f(inputs["norm_pre_mix"]).reshape(1, D),
        "g1": f(inputs["norm_post_mix"]).reshape(1, D),
        "g2": f(inputs["norm_pre_ffn"]).reshape(1, D),
        "g3": f(inputs["norm_post_ffn"]).reshape(1, D),
        "w_in": f(inputs["w_in"])[0],
        "sinks": f(inputs["attn_sinks"]).reshape(1, 16),
        "ln_g": f(inputs["gmlp_ln_g"]).reshape(1, D),
        "ln_b": f(inputs["gmlp_ln_b"]).reshape(1, D),
        "w_s": f(inputs["gmlp_w_s"])[0],
        "b_s": f(inputs["gmlp_b_s"]).reshape(1, 512),
        "wap": f(inputs["w_attn_proj"])[0],
        "wgp": f(inputs["w_gmlp_proj"])[0],
        "wout": f(inputs["w_out"])[0],
        "wfi": f(inputs["w_ffn_in"])[0],
        "wfo": f(inputs["w_ffn_out"])[0],
    }
    in_maps = []
    for c in range(NCORES):
        d = dict(shared)
        d["xp"] = x_prompt[2 * c:2 * c + 2]
        d["xs"] = x_sample[2 * c:2 * c + 2].reshape(32, D)
        d["ck"] = ck[2 * c:2 * c + 2]
        d["cv"] = cv[2 * c:2 * c + 2]
        in_maps.append(d)
    return in_maps


def gather(results):
    cat = lambda k: np.concatenate([np.asarray(r[k]) for r in results], axis=0)
    y_prompt = cat("yp").reshape(16, SEQ, D).astype(np.float32)
    y_sample = cat("ys").reshape(16, 16, D).astype(np.float32)
    kp = cat("kp").reshape(1, 16, 128, 4, 64).astype(np.float32)
    vp = cat("vp").reshape(1, 16, 128, 4, 64).astype(np.float32)
    ks = cat("ks").reshape(1, 16, 128, 4, 64).astype(np.float32)
    vs = cat("vs").reshape(1, 16, 128, 4, 64).astype(np.float32)
    gs = cat("gs").reshape(1, 16, 16, D).astype(np.float32)
    return (y_prompt, y_sample, kp, vp, ks, vs, gs)


def kernel(**inputs):
    nc = _get_nc()
    in_maps = make_in_maps(inputs)
    res = run_bass_kernel_spmd(nc, in_maps, core_ids=list(range(NCORES)))
    return gather(res.results)
```
